# Optimizing a Trainium2 kernel written in Bass

```python
import math
import jax, jax.numpy as jnp
from jax import lax
import numpy as np

D_MODEL = 1024
BATCH = 4
SEQ = 8192
DEPTH = 1
DEC_BATCH = 8
DEC_SEQ = 2048
PAST_LEN = 128

HEAD_DIM = 64
A_HEADS = 4
A_VDIM = 2 * HEAD_DIM
A_WIDTH = A_HEADS * A_VDIM
A_Q_COLS = A_HEADS * 2 * HEAD_DIM
A_K_COLS = A_HEADS * 2 * HEAD_DIM
A_V_COLS = A_HEADS * A_VDIM
B_HEADS = 8
B_KV_HEADS = 2
B_GROUP = B_HEADS // B_KV_HEADS
B_WIDTH = B_HEADS * HEAD_DIM
B_Q_COLS = B_HEADS * HEAD_DIM
B_K_COLS = B_KV_HEADS * HEAD_DIM
B_V_COLS = B_KV_HEADS * HEAD_DIM
MIX_WIDTH = A_WIDTH + B_WIDTH
IN_COLS = A_Q_COLS + A_K_COLS + A_V_COLS + B_Q_COLS + B_K_COLS + B_V_COLS
D_FF = 4 * D_MODEL
PLE_DIM = 256
GRID_W = 64
Q_BLOCK = 128
ROPE_THETA = 10000.0
ROPE_HALF = HEAD_DIM // 2
NORM_EPS = 1e-6
SUBLN_EPS = 1e-5

kernel_name = "hymba_diffattn_axialgqa_encoder"


def _rms_norm(x, g, eps):
    xf = x.astype(jnp.float32)
    y = xf * lax.rsqrt(jnp.mean(xf * xf, axis=-1, keepdims=True) + eps)
    return (y * g.astype(jnp.float32)).astype(x.dtype)


def _alibi_slopes():
    return jnp.asarray([2.0 ** (-8.0 * (h + 1) / A_HEADS) for h in range(A_HEADS)], dtype=jnp.float32)


def _lambda_init(layer_idx):
    return 0.8 - 0.6 * math.exp(-0.3 * layer_idx)


def _to_blocks(t):
    b, s = t.shape[0], t.shape[1]
    return jnp.moveaxis(t.reshape((b, s // Q_BLOCK, Q_BLOCK) + t.shape[2:]), 1, 0)


def _from_blocks(t):
    t = jnp.moveaxis(t, 0, 1)
    return t.reshape((t.shape[0], t.shape[1] * t.shape[2]) + t.shape[3:])


def _diff_attention(q, k, v, lam, slopes):
    s_len = q.shape[1]
    scale = HEAD_DIM ** -0.5
    kpos = jnp.arange(s_len)
    nblk = s_len // Q_BLOCK

    def blk(args):
        qb, i = args
        qpos = i * Q_BLOCK + jnp.arange(Q_BLOCK)
        dist = jnp.abs(qpos[:, None] - kpos[None, :]).astype(jnp.float32)
        sc = jnp.einsum('bqhcd,bkhcd->bhcqk', qb, k) * scale
        sc = sc - slopes[None, :, None, None, None] * dist[None, None, None]
        pr = jax.nn.softmax(sc, axis=-1)
        w = pr[:, :, 0] - lam * pr[:, :, 1]
        return jnp.einsum('bhqk,bkhe->bqhe', w, v)

    out = lax.map(blk, (_to_blocks(q), jnp.arange(nblk)))
    return _from_blocks(out)


def _gqa_attention(q, k, v):
    scale = HEAD_DIM ** -0.5

    def blk(qb):
        sc = jnp.einsum('bqngd,bknd->bngqk', qb, k) * scale
        pr = jax.nn.softmax(sc, axis=-1)
        return jnp.einsum('bngqk,bknd->bqngd', pr, v)

    return _from_blocks(lax.map(blk, _to_blocks(q)))


def _rope_1d(x, ang):
    f = ang.shape[-1]
    x1, x2 = x[..., :f], x[..., f:]
    c = jnp.cos(ang)[:, None, :]
    s = jnp.sin(ang)[:, None, :]
    return jnp.concatenate([x1 * c - x2 * s, x1 * s + x2 * c], axis=-1)


def _axial_angles(s_len):
    rows = s_len // GRID_W
    row_idx = jnp.repeat(jnp.arange(rows), GRID_W).astype(jnp.float32)
    col_idx = jnp.tile(jnp.arange(GRID_W), rows).astype(jnp.float32)
    inv_freq = ROPE_THETA ** (-jnp.arange(0, ROPE_HALF, 2, dtype=jnp.float32) / ROPE_HALF)
    return row_idx[:, None] * inv_freq[None], col_idx[:, None] * inv_freq[None]


def _axial_rope(x, ang_r, ang_c):
    return jnp.concatenate([_rope_1d(x[..., :ROPE_HALF], ang_r), _rope_1d(x[..., ROPE_HALF:], ang_c)], axis=-1)


def _layer(h, p_l, layer_idx, w_in, g_mix, lambda_q1, lambda_k1, lambda_q2, lambda_k2, g_subln,
           g_qnorm, g_knorm, w_out, g_mlp, w_ff1, w_ff2, g_ple, w_ple_gate, w_ple_proj):
    bsz, s_len, _ = h.shape
    f32 = jnp.float32
    n = _rms_norm(h, g_mix, NORM_EPS)
    z = jnp.einsum('bsd,dc->bsc', n, w_in).astype(f32)
    cuts = np.cumsum([A_Q_COLS, A_K_COLS, A_V_COLS, B_Q_COLS, B_K_COLS]).tolist()
    qa, ka, va, qb, kb, vb = jnp.split(z, cuts, axis=-1)

    lam_init = _lambda_init(layer_idx)
    lam = (jnp.exp(jnp.sum(lambda_q1.astype(f32) * lambda_k1.astype(f32)))
           - jnp.exp(jnp.sum(lambda_q2.astype(f32) * lambda_k2.astype(f32))) + lam_init)
    qa = qa.reshape(bsz, s_len, A_HEADS, 2, HEAD_DIM)
    ka = ka.reshape(bsz, s_len, A_HEADS, 2, HEAD_DIM)
    va = va.reshape(bsz, s_len, A_HEADS, A_VDIM)
    oa = _diff_attention(qa, ka, va, lam, _alibi_slopes())
    oa = _rms_norm(oa, g_subln, SUBLN_EPS) * (1.0 - lam_init)
    oa = oa.reshape(bsz, s_len, A_WIDTH)

    ang_r, ang_c = _axial_angles(s_len)
    qb = _rms_norm(qb.reshape(bsz, s_len, B_HEADS, HEAD_DIM), g_qnorm, NORM_EPS)
    kb = _rms_norm(kb.reshape(bsz, s_len, B_KV_HEADS, HEAD_DIM), g_knorm, NORM_EPS)
    qb = _axial_rope(qb, ang_r, ang_c).reshape(bsz, s_len, B_KV_HEADS, B_GROUP, HEAD_DIM)
    kb = _axial_rope(kb, ang_r, ang_c)
    vb = vb.reshape(bsz, s_len, B_KV_HEADS, HEAD_DIM)
    ob = _gqa_attention(qb, kb, vb).reshape(bsz, s_len, B_WIDTH)

    mixed = jnp.concatenate([oa, ob], axis=-1).astype(h.dtype)
    h = h + jnp.einsum('bsc,cd->bsd', mixed, w_out)

    n2 = _rms_norm(h, g_mlp, NORM_EPS)
    a = jax.nn.relu(jnp.einsum('bsd,df->bsf', n2, w_ff1))
    h = h + jnp.einsum('bsf,fd->bsd', a * a, w_ff2)

    gate = jax.nn.sigmoid(jnp.einsum('bsd,de->bse', _rms_norm(h, g_ple, NORM_EPS), w_ple_gate))
    h = h + gate * jnp.einsum('bsp,pd->bsd', p_l, w_ple_proj)
    return h


def _trunk(x, p, w_in, g_mix, lambda_q1, lambda_k1, lambda_q2, lambda_k2, g_subln, g_qnorm, g_knorm,
           w_out, g_mlp, w_ff1, w_ff2, g_ple, w_ple_gate, w_ple_proj, g_final):
    h = x
    for l in range(DEPTH):
        h = _layer(h, p[l], l, w_in[l], g_mix[l], lambda_q1[l], lambda_k1[l], lambda_q2[l], lambda_k2[l],
                   g_subln[l], g_qnorm[l], g_knorm[l], w_out[l], g_mlp[l], w_ff1[l], w_ff2[l],
                   g_ple[l], w_ple_gate[l], w_ple_proj[l])
    return _rms_norm(h, g_final, NORM_EPS)


def setup_inputs(seed: int = 0) -> dict:
    key = jax.random.key(seed)
    ks = jax.random.split(key, 24)
    f32 = jnp.float32

    def nrm(k, shape, scale):
        return jax.random.normal(k, shape, f32) * scale

    def gain(k, shape):
        return 1.0 + 0.02 * jax.random.normal(k, shape, f32)

    return {
        "x_prompt": nrm(ks[0], (BATCH, SEQ, D_MODEL), 1.0),
        "x_sample": nrm(ks[1], (DEC_BATCH, DEC_SEQ, D_MODEL), 1.0),
        "p_prompt": nrm(ks[2], (DEPTH, BATCH, SEQ, PLE_DIM), 1.0),
        "p_sample": nrm(ks[3], (DEPTH, DEC_BATCH, DEC_SEQ, PLE_DIM), 1.0),
        "w_in": nrm(ks[4], (DEPTH, D_MODEL, IN_COLS), D_MODEL ** -0.5),
        "g_mix": gain(ks[5], (DEPTH, D_MODEL)),
        "lambda_q1": nrm(ks[6], (DEPTH, HEAD_DIM), 0.1),
        "lambda_k1": nrm(ks[7], (DEPTH, HEAD_DIM), 0.1),
        "lambda_q2": nrm(ks[8], (DEPTH, HEAD_DIM), 0.1),
        "lambda_k2": nrm(ks[9], (DEPTH, HEAD_DIM), 0.1),
        "g_subln": gain(ks[10], (DEPTH, A_VDIM)),
        "g_qnorm": gain(ks[11], (DEPTH, HEAD_DIM)),
        "g_knorm": gain(ks[12], (DEPTH, HEAD_DIM)),
        "w_out": nrm(ks[13], (DEPTH, MIX_WIDTH, D_MODEL), MIX_WIDTH ** -0.5),
        "g_mlp": gain(ks[14], (DEPTH, D_MODEL)),
        "w_ff1": nrm(ks[15], (DEPTH, D_MODEL, D_FF), D_MODEL ** -0.5),
        "w_ff2": nrm(ks[16], (DEPTH, D_FF, D_MODEL), D_FF ** -0.5),
        "g_ple": gain(ks[17], (DEPTH, D_MODEL)),
        "w_ple_gate": nrm(ks[18], (DEPTH, D_MODEL, D_MODEL), D_MODEL ** -0.5),
        "w_ple_proj": nrm(ks[19], (DEPTH, PLE_DIM, D_MODEL), PLE_DIM ** -0.5),
        "g_final": gain(ks[20], (D_MODEL,)),
    }


def reference(x_prompt, x_sample, p_prompt, p_sample, w_in, g_mix, lambda_q1, lambda_k1, lambda_q2, lambda_k2,
              g_subln, g_qnorm, g_knorm, w_out, g_mlp, w_ff1, w_ff2, g_ple, w_ple_gate, w_ple_proj, g_final):
    y_prompt = _trunk(x_prompt, p_prompt, w_in, g_mix, lambda_q1, lambda_k1, lambda_q2, lambda_k2, g_subln,
                      g_qnorm, g_knorm, w_out, g_mlp, w_ff1, w_ff2, g_ple, w_ple_gate, w_ple_proj, g_final)
    y_sample = _trunk(x_sample, p_sample, w_in, g_mix, lambda_q1, lambda_k1, lambda_q2, lambda_k2, g_subln,
                      g_qnorm, g_knorm, w_out, g_mlp, w_ff1, w_ff2, g_ple, w_ple_gate, w_ple_proj, g_final)
    return (y_prompt, y_sample)
```

```python
import numpy as np
import concourse.bass as bass
import concourse.mybir as mybir
from concourse.bass_utils import run_bass_kernel_spmd

F32 = mybir.dt.float32
BF16 = mybir.dt.bfloat16
AF = mybir.ActivationFunctionType
ALU = mybir.AluOpType
AX = mybir.AxisListType

ENGS = ("pe", "act", "dve", "pool", "sp")
SEM_LIMIT = 30000
DL = (SEM_LIMIT // 16) * 16

D = 1024
DFF = 4096
PLE = 256
SLOPES = [2.0 ** (-8.0 * (h + 1) / 4) for h in range(4)]
LAM_INIT = 0.8 - 0.6 * 1.0


class Res:
    __slots__ = ("name", "last_w", "readers", "excl")

    def __init__(self, name="", excl=False):
        self.name = name
        self.last_w = None
        self.readers = []
        self.excl = excl


class Op:
    __slots__ = ("eng", "fn", "deps", "ddeps", "has_dep", "sig", "dkey", "dcnt")

    def __init__(self, eng, fn):
        self.eng = eng
        self.fn = fn
        self.deps = []
        self.ddeps = {}
        self.has_dep = False
        self.sig = None
        self.dkey = None
        self.dcnt = 0


class Prog:
    def __init__(self, nc):
        self.nc = nc
        self.ops = {e: [] for e in ENGS}
        self.dcount = {}
        self.last_comp = {e: None for e in ENGS}
        self.bar_ops = []
        self.bar_d = {}

    def barrier(self):
        self.bar_ops = [o for o in self.last_comp.values() if o is not None]
        self.bar_d = dict(self.dcount)

    def add(self, eng, fn, reads=(), writes=(), dma=None):
        op = Op(eng, fn)
        deps = list(self.bar_ops)
        for r in reads:
            if r.last_w is not None:
                deps.append(r.last_w)
            if r.excl:
                deps.extend(x for x in r.readers if x.eng != eng)
        for w in writes:
            if w.last_w is not None:
                deps.append(w.last_w)
            deps.extend(w.readers)
        for k, v in self.bar_d.items():
            op.ddeps[k] = v
        seen = set()
        for d in deps:
            if d is op or id(d) in seen:
                continue
            seen.add(id(d))
            if d.dkey is not None:
                k = d.dkey
                op.ddeps[k] = max(op.ddeps.get(k, 0), self.dcount[k])
            else:
                if d.eng == "pe" and eng == "pe":
                    continue
                d.has_dep = True
                op.deps.append(d)
        if dma is not None:
            op.dkey = dma
            self.dcount[dma] = self.dcount.get(dma, 0) + 16
            op.dcnt = self.dcount[dma]
        else:
            self.last_comp[eng] = op
        for r in reads:
            r.readers.append(op)
        for w in writes:
            w.last_w = op
            w.readers = []
        self.ops[eng].append(op)
        return op

    def emit(self):
        nc = self.nc
        esems = {e: [] for e in ENGS}
        for e in ENGS:
            c = 0
            for op in self.ops[e]:
                if op.has_dep and op.dkey is None:
                    c += 1
                    op.sig = c
            nsem = max(1, (c + SEM_LIMIT - 1) // SEM_LIMIT)
            for i in range(nsem):
                esems[e].append(nc.alloc_semaphore(name=f"s_{e}_{i}"))
        dsems = {}
        for k, tot in self.dcount.items():
            n = (tot + DL - 1) // DL
            dsems[k] = [nc.alloc_semaphore(name=f"d_{k}_{i}") for i in range(n)]

        def esem(e, sig):
            i = (sig - 1) // SEM_LIMIT
            return esems[e][i], sig - i * SEM_LIMIT

        def dsem(k, cnt):
            i = (cnt - 1) // DL
            return dsems[k][i], cnt - i * DL

        final_d = dict(self.dcount)

        def run(e, engine):
            waited = {}
            for op in self.ops[e]:
                for d in op.deps:
                    s, v = esem(d.eng, d.sig)
                    key = ("e", d.eng, (d.sig - 1) // SEM_LIMIT)
                    if waited.get(key, 0) < v:
                        engine.wait_ge(s, v)
                        waited[key] = v
                for k, cnt in op.ddeps.items():
                    if cnt <= 0:
                        continue
                    s, v = dsem(k, cnt)
                    key = ("d", k, (cnt - 1) // DL)
                    if waited.get(key, 0) < v:
                        engine.wait_ge(s, v)
                        waited[key] = v
                ins = op.fn(engine)
                if op.dkey is not None:
                    s, v = dsem(op.dkey, op.dcnt)
                    ins.then_inc(s, 16)
                elif op.sig is not None:
                    s, v = esem(e, op.sig)
                    ins.then_inc(s, 1)
            if e == "sp":
                for k, tot in final_d.items():
                    c = 0
                    while c < tot:
                        c = min(tot, c + DL)
                        s, v = dsem(k, c)
                        engine.wait_ge(s, v)

        with nc.Block() as block:
            @block.tensor
            def _(eng):
                run("pe", eng)

            @block.scalar
            def _(eng):
                run("act", eng)

            @block.vector
            def _(eng):
                run("dve", eng)

            @block.gpsimd
            def _(eng):
                run("pool", eng)

            @block.sync
            def _(eng):
                run("sp", eng)


class T:
    __slots__ = ("ap", "r")

    def __init__(self, ap, r=None):
        self.ap = ap
        self.r = r if r is not None else Res()


_DT_SIZE = {F32: 4, BF16: 2}


class Alloc:
    def __init__(self, nc, base=16512, limit=229312):
        self.nc = nc
        self.off = base
        self.limit = limit
        self.n = 0

    def mark(self):
        return self.off

    def release(self, m):
        self.off = m

    def raw(self, shape, dt, off):
        self.n += 1
        return self.nc.alloc_sbuf_tensor_at(f"t{self.n}", list(shape), dt, offset=off).ap()

    def __call__(self, shape, dt, res=None):
        nb = _DT_SIZE[dt]
        for s in shape[1:]:
            nb *= s
        off = (self.off + 31) // 32 * 32
        assert off + nb <= self.limit, f"SBUF overflow {off + nb}"
        self.off = off + nb
        return T(self.raw(shape, dt, off), res)


def build(SKP, SQP, SS, debug=False):
    nc = bass.Bass("TRN2", target_bir_lowering=False)
    P = Prog(nc)
    A = Alloc(nc)

    def din(name, shape, dt=F32):
        return nc.dram_tensor(name, list(shape), dt, kind="ExternalInput").ap()

    def dout(name, shape, dt=F32):
        return nc.dram_tensor(name, list(shape), dt, kind="ExternalOutput").ap()

    def dscr(name, shape, dt=BF16):
        kind = "ExternalOutput" if debug else "Internal"
        return nc.dram_tensor(name, list(shape), dt, kind=kind).ap()

    seqs = [dict(n="p", SK=SKP, SQ=SQP), dict(n="s", SK=SS, SQ=SS)]
    for s in seqs:
        n, SK, SQ = s["n"], s["SK"], s["SQ"]
        s["x"] = din(f"x_{n}", [SK, D])
        s["pl"] = din(f"pl_{n}", [SQ, PLE])
        s["rc"] = din(f"ropec_{n}", [SK, 64])
        s["rs"] = din(f"ropes_{n}", [SK, 64])
        s["y"] = dout(f"y_{n}", [SQ, D])
        s["QTA"] = dscr(f"qta_{n}", [4, 128, SQ])
        s["KTA"] = dscr(f"kta_{n}", [4, 128, SK])
        s["VA"] = dscr(f"va_{n}", [4, 128, SK // 128, 128])
        s["QTB"] = dscr(f"qtb_{n}", [4, 128, SQ])
        s["KTB"] = dscr(f"ktb_{n}", [2, 128, SK])
        s["VB"] = dscr(f"vb_{n}", [2, 128, SK // 128, 65])
        s["MXA"] = dscr(f"mxa_{n}", [4, 128, SQ])
        s["MXB"] = dscr(f"mxb_{n}", [8, 64, SQ])
    w_in = din("w_in", [D, 2304])
    w_out = din("w_out", [D, D])
    w_ff1 = din("w_ff1", [D, DFF])
    w_ff2 = din("w_ff2", [DFF, D])
    w_gate = din("w_gate", [D, D])
    w_proj = din("w_proj", [PLE, D])
    gvec = din("gvec", [4, D])
    lamv = din("lamv", [4, 64])
    gsub = din("gsub", [128, 1])
    gqk = din("gqk", [1, 640])
    ident_d = din("ident", [128, 128])
    BL_d = din("alibi_bl", [128, 4 * 64])
    BR_d = din("alibi_br", [128, 4 * 64])
    FL_d = din("alibi_fl", [128, 4 * 512])
    FR_d = din("alibi_fr", [128, 4 * 512])
    DT_d = din("alibi_dt", [128, 4 * 512])
    THI_d = din("alibi_hi", [128, 4 * 512])
    TLO_d = din("alibi_lo", [128, 4 * 512])
    W1S = dscr("w1s", [4, 128, 8, 1024])
    W2S = dscr("w2s", [4, 128, 8, 1024])

    PS = [nc.alloc_psum_tensor(f"ps{i}", [128, 1024], F32).ap() for i in range(4)]
    RB = [Res(f"bank{i}", excl=True) for i in range(8)]

    def bank(i):
        return PS[i // 2][:, (i % 2) * 512:(i % 2 + 1) * 512]

    def rs_(ts):
        return [t.r if isinstance(t, T) else t for t in ts]

    def dma(eng, key, out, in_, reads=(), writes=()):
        P.add(eng, lambda e: e.dma_start(out=out, in_=in_), rs_(reads), rs_(writes), dma=key)

    def mm(out, lhsT, rhs, start, stop, reads, writes):
        P.add("pe", lambda e: e.matmul(out, lhsT=lhsT, rhs=rhs, start=start, stop=stop),
              rs_(reads), rs_(writes))

    def tr(out, in_, ident, reads, writes):
        P.add("pe", lambda e: e.transpose(out=out, in_=in_, identity=ident), rs_(reads), rs_(writes))

    def act(out, in_, func, reads, writes, bias=None, scale=1.0, accum=None):
        def fn(e):
            kw = dict(out=out, in_=in_, func=func, scale=scale)
            if bias is not None:
                kw["bias"] = bias
            if accum is not None:
                kw["accum_out"] = accum
            return e.activation(**kw)
        P.add("act", fn, rs_(reads), rs_(writes))

    def tt(eng, out, in0, in1, op, reads, writes):
        P.add(eng, lambda e: e.tensor_tensor(out=out, in0=in0, in1=in1, op=op), rs_(reads), rs_(writes))

    def stt(eng, out, in0, scalar, in1, op0, op1, reads, writes):
        P.add(eng, lambda e: e.scalar_tensor_tensor(out=out, in0=in0, scalar=scalar, in1=in1,
                                                    op0=op0, op1=op1), rs_(reads), rs_(writes))

    def ts(eng, out, in0, s1, s2, op0, op1, reads, writes):
        if op1 is None:
            P.add(eng, lambda e: e.tensor_scalar(out=out, in0=in0, scalar1=s1, scalar2=None, op0=op0),
                  rs_(reads), rs_(writes))
        else:
            P.add(eng, lambda e: e.tensor_scalar(out=out, in0=in0, scalar1=s1, scalar2=s2, op0=op0, op1=op1),
                  rs_(reads), rs_(writes))

    def cp(eng, out, in_, reads, writes):
        if eng == "act":
            P.add("act", lambda e: e.copy(out=out, in_=in_), rs_(reads), rs_(writes))
        else:
            P.add(eng, lambda e: e.tensor_copy(out=out, in_=in_), rs_(reads), rs_(writes))

    def memset(eng, ap, val, writes):
        P.add(eng, lambda e: e.memset(ap, val), (), rs_(writes))

    def recip(out, in_, reads, writes):
        P.add("dve", lambda e: e.reciprocal(out=out, in_=in_), rs_(reads), rs_(writes))

    ident_f = A([128, 128], F32)
    ident = A([128, 128], BF16)
    eps6 = A([128, 1], F32)
    eps5 = A([128, 1], F32)
    zero1 = A([128, 1], F32)
    ones_f = A([128, 128], F32)
    onesdiv = A([128, 128], F32)
    e0 = A([128, 33], BF16)
    e32 = A([128, 33], BF16)
    lamt = A([128, 4, 64], F32)
    neglam = A([128, 1], F32)
    gsub08 = A([128, 1], F32)
    lamtmp = A([128, 2, 64], F32)
    lamsum = A([128, 2], F32)
    ssA = A([128, 1], F32)
    rstdA = A([128, 1], F32)

    dma("sp", "c1", ident_f.ap, ident_d, writes=[ident_f])
    cp("dve", ident.ap, ident_f.ap, [ident_f], [ident])
    memset("pool", eps6.ap, 1e-6, [eps6])
    memset("pool", eps5.ap, 1e-5, [eps5])
    memset("pool", zero1.ap, 0.0, [zero1])
    memset("pool", ones_f.ap, 1.0, [ones_f])
    memset("pool", onesdiv.ap, 1.0 / 128.0, [onesdiv])
    memset("pool", e0.ap, 0.0, [e0])
    memset("pool", e0.ap[:, 0:1], 1.0, [e0])
    memset("pool", e32.ap, 0.0, [e32])
    memset("pool", e32.ap[:, 32:33], 1.0, [e32])
    for i in range(4):
        dma("sp", "clam", lamt.ap[:, i, :], lamv[i:i + 1, :].partition_broadcast(128), writes=[lamt])
    tt("dve", lamtmp.ap[:, 0, :], lamt.ap[:, 0, :], lamt.ap[:, 1, :], ALU.mult, [lamt], [lamtmp])
    tt("dve", lamtmp.ap[:, 1, :], lamt.ap[:, 2, :], lamt.ap[:, 3, :], ALU.mult, [lamt], [lamtmp])
    P.add("dve", lambda e: e.tensor_reduce(out=lamsum.ap, in_=lamtmp.ap, axis=AX.X, op=ALU.add),
          [lamtmp.r], [lamsum.r])
    act(lamsum.ap, lamsum.ap, AF.Exp, [lamsum], [lamsum])
    tt("dve", neglam.ap, lamsum.ap[:, 1:2], lamsum.ap[:, 0:1], ALU.subtract, [lamsum], [neglam])
    ts("dve", neglam.ap, neglam.ap, -LAM_INIT, None, ALU.add, None, [neglam], [neglam])
    dma("sp", "c2", gsub08.ap, gsub, writes=[gsub08])
    ts("dve", gsub08.ap, gsub08.ap, 1.0 - LAM_INIT, None, ALU.mult, None, [gsub08], [gsub08])

    e0f = A([128, 33], F32)
    e32f = A([128, 33], F32)
    memset("pool", e0f.ap, 0.0, [e0f])
    memset("pool", e0f.ap[:, 0:1], 1.0, [e0f])
    memset("pool", e32f.ap, 0.0, [e32f])
    memset("pool", e32f.ap[:, 32:33], 1.0, [e32f])
    wstg_f = [A([128, 1024], F32) for _ in range(2)]
    wstg_b = [A([128, 1024], BF16) for _ in range(2)]
    bg_tasks = []

    def _mk_bg(kk, src, dst):
        def task():
            sf, sbb = wstg_f[kk % 2], wstg_b[kk % 2]
            dma("sp", f"bwl{kk % 2}", sf.ap, src, writes=[sf])
            cp("pool", sbb.ap, sf.ap, [sf], [sbb])
            dma("sp", f"bws{kk % 2}", dst, sbb.ap, reads=[sbb])
        return task

    _kk = 0
    for q in range(4):
        for c in range(8):
            bg_tasks.append(_mk_bg(_kk, w_ff1[c * 128:(c + 1) * 128, q * 1024:(q + 1) * 1024], W1S[q, :, c, :]))
            _kk += 1
    for q in range(4):
        for c in range(8):
            r0 = q * 1024 + c * 128
            bg_tasks.append(_mk_bg(_kk, w_ff2[r0:r0 + 128, :], W2S[q, :, c, :]))
            _kk += 1

    base_mark = A.mark()

    def rms_rstd(x_ap, xr, junk, nfree, eps, ss, rstd):
        act(junk.ap, x_ap, AF.Square, [xr], [junk, ss], accum=ss.ap)
        act(rstd.ap, ss.ap, AF.Ln, [ss, eps], [rstd], bias=eps.ap, scale=1.0 / nfree)
        act(rstd.ap, rstd.ap, AF.Exp, [rstd], [rstd], scale=-0.5)

    stg_f = [A([128, 1152], F32) for _ in range(2)]
    win_b = A([128, 8, 2304], BF16)
    gmix = A([128, D], F32)
    dma("sp", "c3", gmix.ap, gvec[0:1, :].partition_broadcast(128), writes=[gmix])
    k = 0
    for c in range(8):
        for hf in range(2):
            sf = stg_f[k % 2]
            dma("sp", f"wld{k % 2}", sf.ap, w_in[c * 128:(c + 1) * 128, hf * 1152:(hf + 1) * 1152], writes=[sf])
            cp("pool", win_b.ap[:, c, hf * 1152:(hf + 1) * 1152], sf.ap, [sf], [win_b])
            k += 1

    a_mark = A.mark()
    NX = 4
    xt = [A([128, D], F32) for _ in range(NX)]
    junkb = A([128, D], BF16)
    ssl = [A([128, 1], F32) for _ in range(NX)]
    rstdl = [A([128, 1], F32) for _ in range(NX)]
    ntl = [A([128, D], BF16) for _ in range(4)]
    nT4 = [A([128, 8, 512], BF16) for _ in range(2)]
    stA = [A([128, 4, 512], BF16) for _ in range(4)]
    vst = [A([128, 4, 512], BF16) for _ in range(2)]
    gqk_t = A([128, 10, 64], F32)
    qkT = [A([128, 6, 512], BF16) for _ in range(2)]
    vbst = [A([128, 4, 2, 65], BF16) for _ in range(2)]
    dma("sp", "c4", gqk_t.ap.rearrange("p h d -> p (h d)"), gqk[0:1, :].partition_broadcast(128), writes=[gqk_t])
    for v in vbst:
        memset("pool", v.ap, 1.0, [v])

    groups = []
    for s in seqs:
        for g in range(s["SK"] // 512):
            groups.append((s, g))
    NSET = 4
    sq = [A([128, 640], F32) for _ in range(NSET)]
    ssq = [A([128, 10], F32) for _ in range(NSET)]
    rq = [A([128, 10], F32) for _ in range(NSET)]
    yb = [A([128, 10, 64], F32) for _ in range(NSET)]
    t1 = [A([128, 10, 64], F32) for _ in range(NSET)]
    t2 = [A([128, 10, 64], F32) for _ in range(NSET)]
    qkb = [A([128, 12, 64], BF16) for _ in range(NSET)]
    ropec = [A([128, 64], F32) for _ in range(NSET)]
    ropes = [A([128, 64], F32) for _ in range(NSET)]
    cnt = dict(tile=0, stA=0, pb=0)
    def a_bank():
        b = cnt["pb"] % 6
        cnt["pb"] += 1
        return b

    def Nelem(gi):
        s, g = groups[gi]
        for t in range(4):
            tok0 = g * 512 + t * 128
            k_ = (gi * 4 + t)
            X = xt[k_ % NX]
            ss_, rstd_ = ssl[k_ % NX], rstdl[k_ % NX]
            dma("sp", f"x{k_ % NX}", X.ap, s["x"][tok0:tok0 + 128, :], writes=[X])
            rms_rstd(X.ap, X, junkb, D, eps6, ss_, rstd_)
            stt("dve", ntl[t].ap, X.ap, rstd_.ap, gmix.ap, ALU.mult, ALU.mult, [X, rstd_, gmix], [ntl[t]])

    def Ntr(gi):
        nT = nT4[gi % 2]
        for t in range(4):
            tb = 6 + (t % 2)
            psT = bank(tb).bitcast(BF16)
            for c in range(8):
                tr(psT[:, c * 128:(c + 1) * 128], ntl[t].ap[:, c * 128:(c + 1) * 128], ident.ap,
                   [ntl[t], ident], [RB[tb]])
            cp("dve", nT.ap[:, :, t * 128:(t + 1) * 128],
               psT[:, 0:1024].rearrange("p (c k) -> p c k", k=128), [RB[tb]], [nT])

    def Bm(gi):
        s, g = groups[gi]
        own = (g * 512) < s["SQ"]
        nT = nT4[gi % 2]
        for t in range(4):
            bq, bkv, ko = t, 4 + t // 2, (t % 2) * 256
            if own:
                for c in range(8):
                    mm(bank(bq), nT.ap[:, c, t * 128:(t + 1) * 128], win_b.ap[:, c, 1536:2048], c == 0, c == 7,
                       [win_b, nT], [RB[bq]])
            for c in range(8):
                mm(bank(bkv)[:, ko:ko + 256], nT.ap[:, c, t * 128:(t + 1) * 128], win_b.ap[:, c, 2048:2304],
                   c == 0, c == 7, [win_b, nT], [RB[bkv]])

    def Bc(gi):
        s, g = groups[gi]
        own = (g * 512) < s["SQ"]
        VB_ = vbst[gi % 2]
        h0 = 0 if own else 8
        nh = 10 - h0
        T4 = range(4)

        def bk(t):
            return t, 4 + t // 2, (t % 2) * 256

        for t in T4:
            tok0 = g * 512 + t * 128
            dma("sp", f"rope{t}", ropec[t].ap, s["rc"][tok0:tok0 + 128, :], writes=[ropec[t]])
            dma("sp", f"rope{t}", ropes[t].ap, s["rs"][tok0:tok0 + 128, :], writes=[ropes[t]])
        for t in T4:
            bq, bkv, ko = bk(t)
            if own:
                act(sq[t].ap[:, 0:512], bank(bq), AF.Square, [RB[bq]], [sq[t]])
            act(sq[t].ap[:, 512:640], bank(bkv)[:, ko:ko + 128], AF.Square, [RB[bkv]], [sq[t]])
        for t in T4:
            P.add("dve", lambda e, o=ssq[t].ap[:, h0:10], i=sq[t].ap[:, h0 * 64:640].rearrange("p (h d) -> p h d", d=64):
                  e.tensor_reduce(out=o, in_=i, axis=AX.X, op=ALU.add), [sq[t].r], [ssq[t].r])
        for t in T4:
            act(rq[t].ap[:, h0:10], ssq[t].ap[:, h0:10], AF.Ln, [ssq[t], eps6], [rq[t]], bias=eps6.ap, scale=1.0 / 64)
        for t in T4:
            act(rq[t].ap[:, h0:10], rq[t].ap[:, h0:10], AF.Exp, [rq[t]], [rq[t]], scale=-0.5)
        for t in T4:
            bq, bkv, ko = bk(t)
            cp("act", VB_.ap[:, t, :, 0:64], bank(bkv)[:, ko + 128:ko + 256].rearrange("p (h d) -> p h d", d=64),
               [RB[bkv]], [VB_])
        for t in T4:
            bq, bkv, ko = bk(t)
            Y, RQ = yb[t], rq[t]
            if own:
                tt("dve", Y.ap[:, 0:8, :], bank(bq).rearrange("p (h d) -> p h d", d=64),
                   RQ.ap[:, 0:8].unsqueeze(2).broadcast_to([128, 8, 64]), ALU.mult, [RB[bq], RQ], [Y])
            tt("dve", Y.ap[:, 8:10, :], bank(bkv)[:, ko:ko + 128].rearrange("p (h d) -> p h d", d=64),
               RQ.ap[:, 8:10].unsqueeze(2).broadcast_to([128, 2, 64]), ALU.mult, [RB[bkv], RQ], [Y])
        for t in T4:
            ysl = yb[t].ap[:, h0:10, :]
            tt("dve", ysl, ysl, gqk_t.ap[:, h0:10, :], ALU.mult, [yb[t], gqk_t], [yb[t]])
        for t in T4:
            ysl = yb[t].ap[:, h0:10, :]
            y5 = ysl.rearrange("p h (a b d) -> p h a b d", a=2, b=2)
            t5 = t2[t].ap[:, h0:10, :].rearrange("p h (a b d) -> p h a b d", a=2, b=2)
            s5 = ropes[t].ap.rearrange("p (a b d) -> p a b d", a=2, b=2)
            for bb in range(2):
                tt("pool", t5[:, :, :, bb, :], y5[:, :, :, 1 - bb, :],
                   s5[:, :, bb, :].unsqueeze(1).broadcast_to([128, nh, 2, 16]), ALU.mult, [yb[t], ropes[t]], [t2[t]])
        for t in T4:
            ysl = yb[t].ap[:, h0:10, :]
            tt("dve", t1[t].ap[:, h0:10, :], ysl, ropec[t].ap.unsqueeze(1).broadcast_to([128, nh, 64]), ALU.mult,
               [yb[t], ropec[t]], [t1[t]])
        for t in T4:
            T1, T2, QKB = t1[t], t2[t], qkb[t]
            if own:
                tt("pool", QKB.ap[:, 0:8, :], T1.ap[:, 0:8, :], T2.ap[:, 0:8, :], ALU.add, [T1, T2], [QKB])
            for dup in range(2):
                tt("pool", QKB.ap[:, 8:12, :].rearrange("p (n u) d -> p n u d", u=2)[:, :, dup, :],
                   T1.ap[:, 8:10, :], T2.ap[:, 8:10, :], ALU.add, [T1, T2], [QKB])

    def Bt(gi):
        s, g = groups[gi]
        own = (g * 512) < s["SQ"]
        G = gi % 2
        QT_, VB_ = qkT[G], vbst[G]
        j0 = 0 if own else 4
        for t in range(4):
            QKB = qkb[t]
            tb = 6 + (t % 2)
            psT = bank(tb).bitcast(BF16)
            for j in range(j0, 6):
                tr(psT[:, j * 128:(j + 1) * 128], QKB.ap[:, 2 * j:2 * j + 2, :].rearrange("p h d -> p (h d)"),
                   ident.ap, [QKB, ident], [RB[tb]])
            cp("dve", QT_.ap[:, j0:6, t * 128:(t + 1) * 128],
               psT[:, j0 * 128:768].rearrange("p (j k) -> p j k", k=128), [RB[tb]], [QT_])
        if own:
            dma("pool", f"qkT{G}", s["QTB"][:, :, g * 512:(g + 1) * 512].rearrange("h p t -> p h t"),
                QT_.ap[:, 0:4, :], reads=[QT_])
        dma("pool", f"qkT{G}", s["KTB"][:, :, g * 512:(g + 1) * 512].rearrange("h p t -> p h t"),
            QT_.ap[:, 4:6, :], reads=[QT_])
        for n in range(2):
            dma("pool", f"vbst{G}", s["VB"][n, :, g * 4:(g + 1) * 4, :], VB_.ap[:, :, n, :], reads=[VB_])

    def Pa(gi):
        s, g = groups[gi]
        own = (g * 512) < s["SQ"]
        G = gi % 2
        nT = nT4[G]
        jobs = []
        if own:
            jobs += [("QTA", h, h * 128) for h in range(4)]
        jobs += [("KTA", h, 512 + h * 128) for h in range(4)]
        for (dst, h, col) in jobs:
            b = a_bank()
            for c in range(8):
                mm(bank(b), win_b.ap[:, c, col:col + 128], nT.ap[:, c, :], c == 0, c == 7,
                   [win_b, nT], [RB[b]])
            k_ = 2 * G + (0 if dst == "QTA" else 1)
            S_ = stA[k_]
            cp("act", S_.ap[:, h, :], bank(b), [RB[b]], [S_])
            if h == 3:
                dma("pool", f"stA{k_}", s[dst][:, :, g * 512:(g + 1) * 512].rearrange("h p t -> p h t"), S_.ap,
                    reads=[S_])
        V = vst[G]
        for t in range(4):
            b = a_bank()
            for c in range(8):
                mm(bank(b), nT.ap[:, c, t * 128:(t + 1) * 128], win_b.ap[:, c, 1024:1536], c == 0, c == 7,
                   [win_b, nT], [RB[b]])
            cp("act", V.ap[:, t, :], bank(b), [RB[b]], [V])
        for h in range(4):
            dma("pool", f"vst{G}", s["VA"][h, :, g * 4:(g + 1) * 4, :], V.ap[:, :, h * 128:(h + 1) * 128],
                reads=[V])

    Nelem(0)
    Ntr(0)
    for gi in range(len(groups)):
        Bm(gi)
        if gi + 1 < len(groups):
            Nelem(gi + 1)
        Bc(gi)
        if gi + 1 < len(groups):
            Ntr(gi + 1)
        Pa(gi)
        Bt(gi)

    P.barrier()
    A.release(base_mark)
    if "A" == debug:
        P.emit()
        return nc

    b_mark = A.mark()
    KTm, QTm, VM = SKP * 2, SQP * 2, (SKP // 128) * 128 * 2
    slots = []
    for i in range(2):
        o_k = (A.mark() + 31) // 32 * 32
        A.off = o_k + KTm
        o_q = A.off
        A.off = o_q + QTm
        o_v = A.off
        A.off = o_v + VM
        slots.append(dict(ok=o_k, oq=o_q, ov=o_v, rk=Res(f"slk{i}"), rq=Res(f"slq{i}"), rv=Res(f"slv{i}")))
    BL = A([128, 4, 64], F32)
    BR = A([128, 4, 64], F32)
    FL = A([128, 4, 512], F32)
    FR = A([128, 4, 512], F32)
    DT = A([128, 4, 512], F32)
    dma("sp", "c5", BL.ap.rearrange("p h m -> p (h m)"), BL_d, writes=[BL])
    dma("sp", "c6", BR.ap.rearrange("p h m -> p (h m)"), BR_d, writes=[BR])
    dma("sp", "c7", FL.ap.rearrange("p h m -> p (h m)"), FL_d, writes=[FL])
    dma("sp", "c8", FR.ap.rearrange("p h m -> p (h m)"), FR_d, writes=[FR])
    PT = [A([128, 1024], BF16) for _ in range(3)]
    thi_b = A([128, 4, 512], BF16)
    tlo_b = A([128, 4, 512], BF16)
    cI = [A([128, 128], BF16) for _ in range(4)]
    for (dst_, src_, key_) in ((thi_b, THI_d, "c20"), (tlo_b, TLO_d, "c21")):
        dma("sp", key_, DT.ap.rearrange("p h m -> p (h m)"), src_, writes=[DT])
        cp("pool", dst_.ap.rearrange("p h m -> p (h m)"), DT.ap.rearrange("p h m -> p (h m)"), [DT], [dst_])
    for h_ in range(4):
        ts("dve", cI[h_].ap, ident_f.ap, 8.0 * SLOPES[h_], None, ALU.mult, None, [ident_f], [cI[h_]])
    acc0 = [A([128, 512], F32) for _ in range(2)]
    acc1 = [A([128, 512], F32) for _ in range(2)]
    accL = [A([33, 512], F32) for _ in range(2)]
    raccs = [A([128, 768], F32) for _ in range(2)]
    tmpA = [A([128, 512], F32) for _ in range(2)]
    tmpL = A([33, 512], F32)
    rL = A([33, 512], F32)
    T0 = A([128, 512], F32)
    T1b = A([128, 512], F32)
    Ot = A([128, 512], F32)
    sqO = A([128, 512], F32)
    rsO = A([128, 512], F32)
    mxs = [A([128, 512], BF16) for _ in range(2)]
    ob0 = [T(acc0[i].ap[0:65, :], acc0[i].r) for i in range(2)]
    ob1 = [T(acc1[i].ap[0:65, :], acc1[i].r) for i in range(2)]
    mxbs = [A([64, 512], BF16) for _ in range(2)]

    ST = [PS[0], PS[1]]
    RST = [[RB[0], RB[1]], [RB[2], RB[3]]]
    OA = [bank(4), bank(5)]
    ROA = [RB[4], RB[5]]
    LA = bank(6)
    RLA = RB[6]
    MISC = bank(7)
    RM = RB[7]

    units = []
    hctr = 0
    segn_ctr = [0]
    for s in seqs:
        SK, SQ = s["SK"], s["SQ"]
        NJ, NI = SK // 128, SQ // 512
        for h in range(4):
            ctx = dict(kind="A", s=s, h=h, slot=slots[hctr % 2], idx=hctr)
            hctr += 1
            first_u = True
            for I in range(NI):
                segs = [("D", [4 * I + d for d in range(4)])]
                if I > 0:
                    segs.append(("L", list(range(0, 4 * I))))
                if 4 * I + 4 < NJ:
                    segs.append(("R", list(range(4 * I + 4, NJ))))
                for si, (sk, js) in enumerate(segs):
                    segn_ctr[0] += 1
                    for ji, J in enumerate(js):
                        units.append(dict(ctx=ctx, segn=segn_ctr[0], I=I, seg=sk, J=J, first=(ji == 0), last=(ji == len(js) - 1),
                                          seg_first=(si == 0), seg_last=(si == len(segs) - 1), load=first_u))
                        first_u = False
        for n in range(2):
            for pr in range(2):
                ctx = dict(kind="B", s=s, n=n, pr=pr, slot=slots[hctr % 2], idx=hctr)
                hctr += 1
                first_u = True
                for I in range(NI):
                    for J in range(NJ):
                        units.append(dict(ctx=ctx, I=I, seg="G", J=J, first=(J == 0), last=(J == NJ - 1),
                                          seg_first=True, seg_last=True, load=first_u))
                        first_u = False

    def head_views(ctx):
        if "kt" in ctx:
            return
        s, sl = ctx["s"], ctx["slot"]
        SK, SQ = s["SK"], s["SQ"]
        ctx["kt"] = A.raw([128, SK], BF16, sl["ok"])
        ctx["qt"] = A.raw([128, SQ], BF16, sl["oq"])
        if ctx["kind"] == "A":
            ctx["v"] = A.raw([128, SK // 128, 128], BF16, sl["ov"])
        else:
            ctx["v"] = A.raw([128, SK // 128, 65], BF16, sl["ov"])

    def load_head(ctx):
        head_views(ctx)
        s, sl = ctx["s"], ctx["slot"]
        key = f"hd{ctx['idx'] % 2}"
        if ctx["kind"] == "A":
            h = ctx["h"]
            dma("sp", key, ctx["kt"], s["KTA"][h], writes=[sl["rk"]])
            dma("sp", key, ctx["qt"], s["QTA"][h], writes=[sl["rq"]])
            dma("sp", key, ctx["v"], s["VA"][h], writes=[sl["rv"]])
        else:
            n, pr = ctx["n"], ctx["pr"]
            dma("sp", key, ctx["kt"], s["KTB"][n], writes=[sl["rk"]])
            dma("sp", key, ctx["qt"], s["QTB"][2 * n + pr], writes=[sl["rq"]])
            dma("sp", key, ctx["v"], s["VB"][n], writes=[sl["rv"]])

    def emit_qk(u, idx):
        ctx = u["ctx"]
        sl = ctx["slot"]
        b = idx % 2
        I, J = u["I"], u["J"]
        kt, qt = ctx["kt"], ctx["qt"]
        diag = (u["seg"] == "D")
        mm(ST[b][:, 0:512], kt[0:64, J * 128:(J + 1) * 128], qt[0:64, I * 512:(I + 1) * 512], True, not diag,
           [sl["rk"], sl["rq"]], [RST[b][0]])
        mm(ST[b][:, 512:1024], kt[64:128, J * 128:(J + 1) * 128], qt[64:128, I * 512:(I + 1) * 512], True, not diag,
           [sl["rk"], sl["rq"]], [RST[b][1]])
        if diag:
            h = ctx["h"]
            d = J - 4 * I
            for c in range(2):
                o = ST[b][:, c * 512:(c + 1) * 512]
                mm(o, cI[h].ap, thi_b.ap[:, d, :], False, False, [cI[h], thi_b], [RST[b][c]])
                mm(o, cI[h].ap, tlo_b.ap[:, d, :], False, True, [cI[h], tlo_b], [RST[b][c]])

    def emit_exp(u, idx):
        ctx = u["ctx"]
        b = idx % 2
        pt = PT[idx % 3]
        I, J = u["I"], u["J"]
        if u["seg"] == "L":
            bias, br = BL.ap[:, ctx["h"], 4 * I - J:4 * I - J + 1], BL
        elif u["seg"] == "R":
            m = J - 4 * I - 4
            bias, br = BR.ap[:, ctx["h"], m:m + 1], BR
        else:
            bias, br = zero1.ap, zero1
        act(pt.ap, ST[b], AF.Exp, RST[b] + [br], [pt], bias=bias, scale=0.125)
        if ctx["kind"] == "A":
            racc = raccs[u["segn"] % 2]
            if u["first"]:
                cp("dve", racc.ap, pt.ap[:, 0:768], [pt], [racc])
            else:
                tt("dve", racc.ap, racc.ap, pt.ap[:, 0:768], ALU.add, [racc, pt], [racc])

    def emit_pv(u, idx):
        ctx = u["ctx"]
        sl = ctx["slot"]
        pt = PT[idx % 3]
        J = u["J"]
        v = ctx["v"]
        if ctx["kind"] == "A":
            for c in range(2):
                mm(OA[c], v[:, J, :], pt.ap[:, c * 512:(c + 1) * 512], u["first"], u["last"],
                   [sl["rv"], pt], [ROA[c]])
            mm(LA[0:33, 256:512], e32.ap, pt.ap[:, 768:1024], u["first"], False, [e32, pt], [RLA])
            if u["last"]:
                racc = raccs[u["segn"] % 2]
                mm(LA[0:33, :], e0f.ap, racc.ap[:, 0:512], False, False, [e0f, racc], [RLA])
                mm(LA[0:33, 0:256], e32f.ap, racc.ap[:, 512:768], False, True, [e32f, racc], [RLA])
        else:
            for c in range(2):
                mm(OA[c][0:65, :], v[:, J, :], pt.ap[:, c * 512:(c + 1) * 512], u["first"], u["last"],
                   [sl["rv"], pt], [ROA[c]])

    deferred = []
    cur = [0]

    def defer(k, fn):
        deferred.append((cur[0] + k, fn))

    def run_deferred(force=False):
        keep = []
        for (due, fn) in deferred:
            if force or due <= cur[0]:
                fn()
            else:
                keep.append((due, fn))
        deferred[:] = keep

    qb_ctr = [0]

    def seg_end_A(u):
        ctx = u["ctx"]
        h = ctx["h"]
        sk = u["seg"]
        pq = qb_ctr[0] % 2
        a0, a1, aL = acc0[pq], acc1[pq], accL[pq]
        if sk == "D":
            cp("dve", a0.ap, OA[0], [ROA[0]], [a0])
            cp("act", a1.ap, OA[1], [ROA[1]], [a1])
            cp("dve", aL.ap, LA[0:33, :], [RLA], [aL])
        else:
            Ft = FL if sk == "L" else FR
            tt("dve", tmpA[0].ap, OA[0], Ft.ap[:, h, :], ALU.mult, [ROA[0], Ft], [tmpA[0]])
            cp("act", tmpA[1].ap, OA[1], [ROA[1]], [tmpA[1]])
            tt("dve", tmpL.ap, LA[0:33, :], Ft.ap[0:33, h, :], ALU.mult, [RLA, Ft], [tmpL])
            tt("pool", a0.ap, a0.ap, tmpA[0].ap, ALU.add, [a0, tmpA[0]], [a0])
            tt("pool", tmpA[1].ap, tmpA[1].ap, Ft.ap[:, h, :], ALU.mult, [tmpA[1], Ft], [tmpA[1]])
            tt("pool", a1.ap, a1.ap, tmpA[1].ap, ALU.add, [a1, tmpA[1]], [a1])
            tt("pool", aL.ap, aL.ap, tmpL.ap, ALU.add, [aL, tmpL], [aL])

    def epilogue_A(u):
        ctx = u["ctx"]
        h, I, s = ctx["h"], u["I"], ctx["s"]
        run_deferred(force=True)
        pq = qb_ctr[0] % 2
        qb_ctr[0] += 1
        a0, a1, aL = acc0[pq], acc1[pq], accL[pq]
        mx = mxs[pq]

        def s0():
            act(rL.ap, aL.ap, AF.Ln, [aL], [rL])
            act(rL.ap, rL.ap, AF.Exp, [rL], [rL], scale=-1.0)

        def s1():
            mm(MISC, ones_f.ap[0:1, :], rL.ap[0:1, :], True, True, [ones_f, rL], [RM])
            tt("dve", T0.ap, a0.ap, MISC, ALU.mult, [a0, RM], [T0])

        def s2():
            mm(MISC, ones_f.ap[32:33, :], rL.ap[32:33, :], True, True, [ones_f, rL], [RM])
            tt("dve", T1b.ap, a1.ap, MISC, ALU.mult, [a1, RM], [T1b])
            stt("dve", Ot.ap, T1b.ap, neglam.ap, T0.ap, ALU.mult, ALU.add, [T1b, T0, neglam], [Ot])

        def s3():
            act(sqO.ap, Ot.ap, AF.Square, [Ot], [sqO])

        def s4():
            mm(MISC, onesdiv.ap, sqO.ap, True, True, [onesdiv, sqO], [RM])

        def s5():
            act(rsO.ap, MISC, AF.Ln, [RM, eps5], [rsO], bias=eps5.ap, scale=1.0)
            act(rsO.ap, rsO.ap, AF.Exp, [rsO], [rsO], scale=-0.5)
            stt("dve", mx.ap, Ot.ap, gsub08.ap, rsO.ap, ALU.mult, ALU.mult, [Ot, gsub08, rsO], [mx])
            dma("pool", f"mx{pq}", s["MXA"][h, :, I * 512:(I + 1) * 512], mx.ap, reads=[mx])

        defer(4, s0)
        defer(8, s1)
        defer(10, s2)
        defer(12, s3)
        defer(14, s4)
        defer(16, s5)

    def epilogue_B(u):
        ctx = u["ctx"]
        n, pr, I, s = ctx["n"], ctx["pr"], u["I"], ctx["s"]
        run_deferred(force=True)
        pq = qb_ctr[0] % 2
        qb_ctr[0] += 1
        obs = [ob0[pq], ob1[pq]]
        cp("act", obs[0].ap, OA[0][0:65, :], [ROA[0]], [obs[0]])
        cp("dve", obs[1].ap, OA[1][0:65, :], [ROA[1]], [obs[1]])
        for c in range(2):
            head = 4 * n + 2 * pr + c
            ob = obs[c]
            mxb = mxbs[c]

            def s0(ob=ob):
                recip(ob.ap[64:65, :], ob.ap[64:65, :], [ob], [ob])

            def s1(ob=ob, mxb=mxb, head=head, c=c):
                mm(MISC[0:64, :], ones_f.ap[64:65, 0:64], ob.ap[64:65, :], True, True, [ones_f, ob], [RM])
                tt("dve", mxb.ap, ob.ap[0:64, :], MISC[0:64, :], ALU.mult, [ob, RM], [mxb])
                dma("pool", f"mxb{c}", s["MXB"][head, :, I * 512:(I + 1) * 512], mxb.ap, reads=[mxb])

            defer(2 + 4 * c, s0)
            defer(6 + 4 * c, s1)

    NU = len(units)
    ctxs = []
    for u in units:
        if u["load"]:
            ctxs.append(u["ctx"])
    bg_every = max(1, NU // (len(bg_tasks) + 1))
    LAG = 2
    for idx in range(NU + LAG):
        cur[0] = idx
        run_deferred()
        if idx % bg_every == bg_every - 1 and bg_tasks:
            bg_tasks.pop(0)()
        if idx < NU:
            u = units[idx]
            if u["load"] and u["ctx"]["idx"] == 0:
                load_head(ctxs[0])
            emit_qk(u, idx)
            emit_exp(u, idx)
        if idx >= LAG:
            u = units[idx - LAG]
            emit_pv(u, idx - LAG)
            if u["last"]:
                if u["ctx"]["kind"] == "A":
                    seg_end_A(u)
                    if u["seg_last"]:
                        epilogue_A(u)
                else:
                    epilogue_B(u)
        j = idx - (LAG - 1)
        if 0 <= j < NU and units[j]["load"]:
            ci = units[j]["ctx"]["idx"]
            if ci + 1 < len(ctxs):
                load_head(ctxs[ci + 1])

    run_deferred(force=True)
    while bg_tasks:
        bg_tasks.pop(0)()
    P.barrier()
    A.release(base_mark)
    if "B" == debug:
        P.emit()
        return nc

    wout_a = A([128, 8, D], BF16)
    wgate = A([128, 8, D], BF16)
    wproj = A([128, 2, D], BF16)
    gmlp = A([128, D], F32)
    gple = A([128, D], F32)
    gfin = A([128, D], F32)
    dma("sp", "c10", gmlp.ap, gvec[1:2, :].partition_broadcast(128), writes=[gmlp])
    dma("sp", "c11", gple.ap, gvec[2:3, :].partition_broadcast(128), writes=[gple])
    dma("sp", "c12", gfin.ap, gvec[3:4, :].partition_broadcast(128), writes=[gfin])
    cstg = wstg_f
    k = 0
    for c in range(8):
        sf = cstg[k % 2]
        dma("sp", f"cw{k % 2}", sf.ap, w_out[c * 128:(c + 1) * 128, :], writes=[sf])
        cp("pool", wout_a.ap[:, c, :], sf.ap, [sf], [wout_a])
        k += 1
    for c in range(8):
        sf = cstg[k % 2]
        dma("sp", f"cw{k % 2}", sf.ap, w_gate[c * 128:(c + 1) * 128, :], writes=[sf])
        cp("pool", wgate.ap[:, c, :], sf.ap, [sf], [wgate])
        k += 1
    for c in range(2):
        sf = cstg[k % 2]
        dma("sp", f"cw{k % 2}", sf.ap, w_proj[c * 128:(c + 1) * 128, :], writes=[sf])
        cp("pool", wproj.ap[:, c, :], sf.ap, [sf], [wproj])
        k += 1

    w1q = [A([128, 8, 512], BF16) for _ in range(2)]
    w2q = [A([128, 4, 1024], BF16) for _ in range(2)]
    xh = [[A([128, D], F32) for _ in range(4)] for _ in range(2)]
    mxa = [A([128, 4, 512], BF16) for _ in range(2)]
    mxb_ = [A([128, 4, 512], BF16) for _ in range(2)]
    plt = A([128, 4, PLE], F32)
    plb = A([128, PLE], BF16)
    pT = A([128, 2, 512], BF16)
    n2 = wstg_b
    n2T = A([128, 8, 512], BF16)
    rl_ = [A([128, 512], F32) for _ in range(2)]
    a2T = [A([128, 4, 512], BF16) for _ in range(2)]
    junkc = A([128, D], BF16)
    ssc = [A([128, 1], F32) for _ in range(2)]
    rstdc = [A([128, 1], F32) for _ in range(2)]
    egate = [A([128, D], F32) for _ in range(4)]
    yout = cstg

    PA = [PS[0], PS[1]]
    RPA = [[RB[0], RB[1]], [RB[2], RB[3]]]
    pa_ctr = 0
    pf_ctr = 0
    wq_ctr = 0
    nrm_ctr = 0
    gctr = 0

    def norm_T(Xs, gt, dstT):
        nonlocal nrm_ctr
        for t in range(4):
            i2 = nrm_ctr % 2
            nrm_ctr += 1
            rms_rstd(Xs[t].ap, Xs[t], junkc, D, eps6, ssc[i2], rstdc[i2])
            stt("dve", n2[i2].ap, Xs[t].ap, rstdc[i2].ap, gt.ap, ALU.mult, ALU.mult, [Xs[t], rstdc[i2], gt], [n2[i2]])
            tb = 6 + (t % 2)
            psT = bank(tb).bitcast(BF16)
            for c in range(8):
                tr(psT[:, c * 128:(c + 1) * 128], n2[i2].ap[:, c * 128:(c + 1) * 128], ident.ap,
                   [n2[i2], ident], [RB[tb]])
            cp("dve", dstT.ap[:, :, t * 128:(t + 1) * 128],
               psT[:, 0:1024].rearrange("p (c k) -> p c k", k=128), [RB[tb]], [dstT])

    def ff1(W1, AT):
        nonlocal pf_ctr
        for fc in range(4):
            pf = 4 + (pf_ctr % 2)
            RLt = rl_[pf_ctr % 2]
            pf_ctr += 1
            for c in range(8):
                mm(bank(pf), W1.ap[:, c, fc * 128:(fc + 1) * 128], n2T.ap[:, c, :], c == 0, c == 7,
                   [W1, n2T], [RB[pf]])
            act(RLt.ap, bank(pf), AF.Relu, [RB[pf]], [RLt])
            tt("dve", AT.ap[:, fc, :], RLt.ap, RLt.ap, ALU.mult, [RLt], [AT])

    def ff2(W2, AT, Xs):
        nonlocal pa_ctr
        for t in range(4):
            pa = pa_ctr % 2
            pa_ctr += 1
            for nh_ in range(2):
                o = PA[pa][:, nh_ * 512:(nh_ + 1) * 512]
                for fc in range(4):
                    mm(o, AT.ap[:, fc, t * 128:(t + 1) * 128], W2.ap[:, fc, nh_ * 512:(nh_ + 1) * 512],
                       fc == 0, fc == 3, [AT, W2], [RPA[pa][nh_]])
            tt("dve", Xs[t].ap, PA[pa], Xs[t].ap, ALU.add, RPA[pa] + [Xs[t]], [Xs[t]])

    for s in seqs:
        SQ = s["SQ"]
        for g in range(SQ // 512):
            G = gctr % 2
            gctr += 1
            Xs, MA, MB, PL = xh[G], mxa[G], mxb_[G], plt
            for t in range(4):
                tok0 = g * 512 + t * 128
                dma("sp", f"cx{G}", Xs[t].ap, s["x"][tok0:tok0 + 128, :], writes=[Xs[t]])
            for hh in range(4):
                dma("sp", f"cm{G}", MA.ap[:, hh, :], s["MXA"][hh, :, g * 512:(g + 1) * 512], writes=[MA])
            mxb_v = s["MXB"].rearrange("(p2 c) e t -> p2 (c e) t", c=2)
            for hh in range(4):
                dma("sp", f"cm{G}", MB.ap[:, hh, :], mxb_v[hh, :, g * 512:(g + 1) * 512], writes=[MB])
            dma("sp", "cpl", PL.ap, s["pl"][g * 512:(g + 1) * 512, :].rearrange("(t p) e -> p t e", p=128),
                writes=[PL])
            for t in range(4):
                cp("pool", plb.ap, PL.ap[:, t, :], [PL], [plb])
                psT = bank(7).bitcast(BF16)
                for c in range(2):
                    tr(psT[:, c * 128:(c + 1) * 128], plb.ap[:, c * 128:(c + 1) * 128], ident.ap, [plb, ident], [RB[7]])
                cp("dve", pT.ap[:, :, t * 128:(t + 1) * 128], psT[:, 0:256].rearrange("p (c k) -> p c k", k=128),
                   [RB[7]], [pT])
            for t in range(4):
                pa = pa_ctr % 2
                pa_ctr += 1
                for nh_ in range(2):
                    o = PA[pa][:, nh_ * 512:(nh_ + 1) * 512]
                    for c in range(4):
                        mm(o, MA.ap[:, c, t * 128:(t + 1) * 128], wout_a.ap[:, c, nh_ * 512:(nh_ + 1) * 512],
                           c == 0, False, [MA, wout_a], [RPA[pa][nh_]])
                    for c in range(4):
                        mm(o, MB.ap[:, c, t * 128:(t + 1) * 128], wout_a.ap[:, 4 + c, nh_ * 512:(nh_ + 1) * 512],
                           False, c == 3, [MB, wout_a], [RPA[pa][nh_]])
                tt("dve", Xs[t].ap, PA[pa], Xs[t].ap, ALU.add, RPA[pa] + [Xs[t]], [Xs[t]])
            norm_T(Xs, gmlp, n2T)
            ws = []
            for q8 in range(8):
                q4, hf = q8 // 2, q8 % 2
                W1, W2, AT = w1q[wq_ctr % 2], w2q[wq_ctr % 2], a2T[wq_ctr % 2]
                dma("sp", f"w1q{wq_ctr % 2}", W1.ap, W1S[q4][:, :, hf * 512:(hf + 1) * 512], writes=[W1])
                dma("sp", f"w2q{wq_ctr % 2}", W2.ap, W2S[q4][:, hf * 4:(hf + 1) * 4, :], writes=[W2])
                wq_ctr += 1
                ff1(W1, AT)
                if ws:
                    ff2(ws[-1][0], ws[-1][1], Xs)
                ws.append((W2, AT))
            ff2(ws[-1][0], ws[-1][1], Xs)
            norm_T(Xs, gple, n2T)
            for t in range(4):
                pa = pa_ctr % 2
                pa_ctr += 1
                EG = egate[t]
                for nh_ in range(2):
                    o = PA[pa][:, nh_ * 512:(nh_ + 1) * 512]
                    for c in range(8):
                        mm(o, n2T.ap[:, c, t * 128:(t + 1) * 128], wgate.ap[:, c, nh_ * 512:(nh_ + 1) * 512],
                           c == 0, c == 7, [n2T, wgate], [RPA[pa][nh_]])
                act(EG.ap, PA[pa], AF.Sigmoid, RPA[pa], [EG])
            for t in range(4):
                EG = egate[t]
                pa = pa_ctr % 2
                pa_ctr += 1
                for nh_ in range(2):
                    o = PA[pa][:, nh_ * 512:(nh_ + 1) * 512]
                    for c in range(2):
                        mm(o, pT.ap[:, c, t * 128:(t + 1) * 128], wproj.ap[:, c, nh_ * 512:(nh_ + 1) * 512],
                           c == 0, c == 1, [pT, wproj], [RPA[pa][nh_]])
                tt("dve", EG.ap, PA[pa], EG.ap, ALU.mult, RPA[pa] + [EG], [EG])
                tt("pool", Xs[t].ap, Xs[t].ap, EG.ap, ALU.add, [Xs[t], EG], [Xs[t]])
            for t in range(4):
                i2 = nrm_ctr % 2
                nrm_ctr += 1
                YO = yout[t % 2]
                rms_rstd(Xs[t].ap, Xs[t], junkc, D, eps6, ssc[i2], rstdc[i2])
                stt("dve", YO.ap, Xs[t].ap, rstdc[i2].ap, gfin.ap, ALU.mult, ALU.mult, [Xs[t], rstdc[i2], gfin], [YO])
                tok0 = g * 512 + t * 128
                dma("pool", f"yo{t % 2}", s["y"][tok0:tok0 + 128, :], YO.ap, reads=[YO])

    P.emit()
    return nc


def _rope_tables(pos):
    f32 = np.float32
    row = (pos // 64).astype(f32)
    col = (pos % 64).astype(f32)
    inv = (f32(10000.0) ** (-np.arange(0, 32, 2, dtype=f32) / f32(32))).astype(f32)
    ar = (row[:, None] * inv[None]).astype(f32)
    ac = (col[:, None] * inv[None]).astype(f32)
    cr, sr, cc, sc = np.cos(ar), np.sin(ar), np.cos(ac), np.sin(ac)
    C = np.concatenate([cr, cr, cc, cc], axis=1).astype(f32)
    S = np.concatenate([-sr, sr, -sc, sc], axis=1).astype(f32)
    return np.ascontiguousarray(C), np.ascontiguousarray(S)


def _alibi_tables():
    f32 = np.float32
    p = np.arange(128, dtype=np.float64)[:, None]
    x = np.arange(512, dtype=np.float64)[None, :]
    BL = np.zeros((128, 4, 64), f32)
    BR = np.zeros((128, 4, 64), f32)
    FL = np.zeros((128, 4, 512), f32)
    FR = np.zeros((128, 4, 512), f32)
    DT = np.zeros((128, 4, 512), f32)
    m = np.arange(64, dtype=np.float64)[None, :]
    for h, sl in enumerate(SLOPES):
        BL[:, h, :] = -sl * (128.0 * m - p)
        BR[:, h, :] = -sl * (128.0 * m + p + 1.0)
        FL[:, h, :] = np.exp(-sl * x)
        FR[:, h, :] = np.exp(-sl * (511.0 - x))
    THI = np.zeros((128, 4, 512), f32)
    TLO = np.zeros((128, 4, 512), f32)
    for d in range(4):
        dist = np.abs(x - (128.0 * d + p))
        DT[:, d, :] = dist
        THI[:, d, :] = -4.0 * np.floor(dist / 4.0)
        TLO[:, d, :] = -(dist - 4.0 * np.floor(dist / 4.0))
    return [np.ascontiguousarray(a.reshape(128, -1)) for a in (BL, BR, FL, FR, DT, THI, TLO)]


_NC_CACHE = {}


def make_in_maps(inputs):
    xp, xs = np.asarray(inputs["x_prompt"]), np.asarray(inputs["x_sample"])
    pp, psm = np.asarray(inputs["p_prompt"])[0], np.asarray(inputs["p_sample"])[0]
    SKP, SS = xp.shape[1], xs.shape[1]
    SQP = SKP // 2
    f32 = np.float32
    BLt, BRt, FLt, FRt, DTt, THIt, TLOt = _alibi_tables()
    gvec = np.stack([np.asarray(inputs["g_mix"])[0], np.asarray(inputs["g_mlp"])[0],
                     np.asarray(inputs["g_ple"])[0], np.asarray(inputs["g_final"])]).astype(f32)
    lamv = np.stack([np.asarray(inputs[k])[0] for k in ("lambda_q1", "lambda_k1", "lambda_q2", "lambda_k2")]).astype(f32)
    gq, gk = np.asarray(inputs["g_qnorm"])[0], np.asarray(inputs["g_knorm"])[0]
    gqk = np.concatenate([np.tile(gq, 8), np.tile(gk, 2)])[None, :].astype(f32)
    common = {
        "w_in": np.ascontiguousarray(np.asarray(inputs["w_in"])[0]),
        "w_out": np.ascontiguousarray(np.asarray(inputs["w_out"])[0]),
        "w_ff1": np.ascontiguousarray(np.asarray(inputs["w_ff1"])[0]),
        "w_ff2": np.ascontiguousarray(np.asarray(inputs["w_ff2"])[0]),
        "w_gate": np.ascontiguousarray(np.asarray(inputs["w_ple_gate"])[0]),
        "w_proj": np.ascontiguousarray(np.asarray(inputs["w_ple_proj"])[0]),
        "gvec": gvec, "lamv": lamv,
        "gsub": np.ascontiguousarray(np.asarray(inputs["g_subln"])[0][:, None].astype(f32)),
        "gqk": gqk, "ident": np.eye(128, dtype=f32),
        "alibi_bl": BLt, "alibi_br": BRt, "alibi_fl": FLt, "alibi_fr": FRt, "alibi_dt": DTt, "alibi_hi": THIt, "alibi_lo": TLOt,
    }
    pos_nat = np.arange(SKP)
    Cn, Sn = _rope_tables(pos_nat)
    Cr, Sr = _rope_tables(pos_nat[::-1])
    Cs, Ss_ = _rope_tables(np.arange(SS))
    in_maps = []
    for c in range(8):
        b, hf = c // 2, c % 2
        if hf == 0:
            xl, pl = xp[b], pp[b][:SQP]
            rc, rs = Cn, Sn
        else:
            xl, pl = xp[b][::-1], pp[b][::-1][:SQP]
            rc, rs = Cr, Sr
        m = dict(common)
        m.update({
            "x_p": np.ascontiguousarray(xl), "pl_p": np.ascontiguousarray(pl), "ropec_p": rc, "ropes_p": rs,
            "x_s": np.ascontiguousarray(xs[c]), "pl_s": np.ascontiguousarray(psm[c]), "ropec_s": Cs, "ropes_s": Ss_,
        })
        in_maps.append(m)
    return in_maps, (SKP, SQP, SS)


def kernel(**inputs):
    in_maps, cfg = make_in_maps(inputs)
    SKP, SQP, SS = cfg
    if cfg not in _NC_CACHE:
        _NC_CACHE[cfg] = build(SKP, SQP, SS)
    nc = _NC_CACHE[cfg]
    res = run_bass_kernel_spmd(nc, in_maps, core_ids=list(range(8)))
    yp = np.zeros((4, SKP, D), np.float32)
    ys = np.zeros((8, SS, D), np.float32)
    for c in range(8):
        r = res.results[c]
        b, hf = c // 2, c % 2
        if hf == 0:
            yp[b, :SQP] = r["y_p"]
        else:
            yp[b, SQP:] = r["y_p"][::-1]
        ys[c] = r["y_s"]
    return yp, ys
```

```python
import numpy as np
import concourse.bass as bass
import concourse.mybir as mybir
from concourse.bass_utils import run_bass_kernel_spmd

F32 = mybir.dt.float32
BF16 = mybir.dt.bfloat16
AF = mybir.ActivationFunctionType
ALU = mybir.AluOpType
AX = mybir.AxisListType

ENGS = ("pe", "act", "dve", "pool", "sp")
SEM_LIMIT = 30000
DL = (SEM_LIMIT // 16) * 16

D = 1024
DFF = 4096
PLE = 256
SLOPES = [2.0 ** (-8.0 * (h + 1) / 4) for h in range(4)]
LAM_INIT = 0.8 - 0.6 * 1.0


class Res:
    __slots__ = ("name", "last_w", "readers", "excl")

    def __init__(self, name="", excl=False):
        self.name = name
        self.last_w = None
        self.readers = []
        self.excl = excl


class Op:
    __slots__ = ("eng", "fn", "deps", "ddeps", "has_dep", "sig", "dkey", "dcnt")

    def __init__(self, eng, fn):
        self.eng = eng
        self.fn = fn
        self.deps = []
        self.ddeps = {}
        self.has_dep = False
        self.sig = None
        self.dkey = None
        self.dcnt = 0


class Prog:
    def __init__(self, nc):
        self.nc = nc
        self.ops = {e: [] for e in ENGS}
        self.dcount = {}
        self.last_comp = {e: None for e in ENGS}
        self.bar_ops = []
        self.bar_d = {}

    def barrier(self):
        self.bar_ops = [o for o in self.last_comp.values() if o is not None]
        self.bar_d = dict(self.dcount)

    def add(self, eng, fn, reads=(), writes=(), dma=None):
        op = Op(eng, fn)
        deps = list(self.bar_ops)
        for r in reads:
            if r.last_w is not None:
                deps.append(r.last_w)
            if r.excl:
                deps.extend(x for x in r.readers if x.eng != eng)
        for w in writes:
            if w.last_w is not None:
                deps.append(w.last_w)
            deps.extend(w.readers)
        for k, v in self.bar_d.items():
            op.ddeps[k] = v
        seen = set()
        for d in deps:
            if d is op or id(d) in seen:
                continue
            seen.add(id(d))
            if d.dkey is not None:
                k = d.dkey
                op.ddeps[k] = max(op.ddeps.get(k, 0), self.dcount[k])
            else:
                if d.eng == "pe" and eng == "pe":
                    continue
                d.has_dep = True
                op.deps.append(d)
        if dma is not None:
            op.dkey = dma
            self.dcount[dma] = self.dcount.get(dma, 0) + 16
            op.dcnt = self.dcount[dma]
        else:
            self.last_comp[eng] = op
        for r in reads:
            r.readers.append(op)
        for w in writes:
            w.last_w = op
            w.readers = []
        self.ops[eng].append(op)
        return op

    def emit(self):
        nc = self.nc
        esems = {e: [] for e in ENGS}
        for e in ENGS:
            c = 0
            for op in self.ops[e]:
                if op.has_dep and op.dkey is None:
                    c += 1
                    op.sig = c
            nsem = max(1, (c + SEM_LIMIT - 1) // SEM_LIMIT)
            for i in range(nsem):
                esems[e].append(nc.alloc_semaphore(name=f"s_{e}_{i}"))
        dsems = {}
        for k, tot in self.dcount.items():
            n = (tot + DL - 1) // DL
            dsems[k] = [nc.alloc_semaphore(name=f"d_{k}_{i}") for i in range(n)]

        def esem(e, sig):
            i = (sig - 1) // SEM_LIMIT
            return esems[e][i], sig - i * SEM_LIMIT

        def dsem(k, cnt):
            i = (cnt - 1) // DL
            return dsems[k][i], cnt - i * DL

        final_d = dict(self.dcount)

        def run(e, engine):
            waited = {}
            for op in self.ops[e]:
                for d in op.deps:
                    s, v = esem(d.eng, d.sig)
                    key = ("e", d.eng, (d.sig - 1) // SEM_LIMIT)
                    if waited.get(key, 0) < v:
                        engine.wait_ge(s, v)
                        waited[key] = v
                for k, cnt in op.ddeps.items():
                    if cnt <= 0:
                        continue
                    s, v = dsem(k, cnt)
                    key = ("d", k, (cnt - 1) // DL)
                    if waited.get(key, 0) < v:
                        engine.wait_ge(s, v)
                        waited[key] = v
                ins = op.fn(engine)
                if op.dkey is not None:
                    s, v = dsem(op.dkey, op.dcnt)
                    ins.then_inc(s, 16)
                elif op.sig is not None:
                    s, v = esem(e, op.sig)
                    ins.then_inc(s, 1)
            if e == "sp":
                for k, tot in final_d.items():
                    c = 0
                    while c < tot:
                        c = min(tot, c + DL)
                        s, v = dsem(k, c)
                        engine.wait_ge(s, v)

        with nc.Block() as block:
            @block.tensor
            def _(eng):
                run("pe", eng)

            @block.scalar
            def _(eng):
                run("act", eng)

            @block.vector
            def _(eng):
                run("dve", eng)

            @block.gpsimd
            def _(eng):
                run("pool", eng)

            @block.sync
            def _(eng):
                run("sp", eng)


class T:
    __slots__ = ("ap", "r")

    def __init__(self, ap, r=None):
        self.ap = ap
        self.r = r if r is not None else Res()


_DT_SIZE = {F32: 4, BF16: 2}


class Alloc:
    def __init__(self, nc, base=16512, limit=229312):
        self.nc = nc
        self.off = base
        self.limit = limit
        self.n = 0

    def mark(self):
        return self.off

    def release(self, m):
        self.off = m

    def raw(self, shape, dt, off):
        self.n += 1
        return self.nc.alloc_sbuf_tensor_at(f"t{self.n}", list(shape), dt, offset=off).ap()

    def __call__(self, shape, dt, res=None):
        nb = _DT_SIZE[dt]
        for s in shape[1:]:
            nb *= s
        off = (self.off + 31) // 32 * 32
        assert off + nb <= self.limit, f"SBUF overflow {off + nb}"
        self.off = off + nb
        return T(self.raw(shape, dt, off), res)


def build(SKP, SQP, SS, debug=False):
    nc = bass.Bass("TRN2", target_bir_lowering=False)
    P = Prog(nc)
    A = Alloc(nc)

    def din(name, shape, dt=F32):
        return nc.dram_tensor(name, list(shape), dt, kind="ExternalInput").ap()

    def dout(name, shape, dt=F32):
        return nc.dram_tensor(name, list(shape), dt, kind="ExternalOutput").ap()

    def dscr(name, shape, dt=BF16):
        kind = "ExternalOutput" if debug else "Internal"
        return nc.dram_tensor(name, list(shape), dt, kind=kind).ap()

    seqs = [dict(n="p", SK=SKP, SQ=SQP), dict(n="s", SK=SS, SQ=SS)]
    for s in seqs:
        n, SK, SQ = s["n"], s["SK"], s["SQ"]
        s["x"] = din(f"x_{n}", [SK, D])
        s["pl"] = din(f"pl_{n}", [SQ, PLE])
        s["rc"] = din(f"ropec_{n}", [SK, 64])
        s["rs"] = din(f"ropes_{n}", [SK, 64])
        s["y"] = dout(f"y_{n}", [SQ, D])
        s["QTA"] = dscr(f"qta_{n}", [4, 128, SQ])
        s["KTA"] = dscr(f"kta_{n}", [4, 128, SK])
        s["VA"] = dscr(f"va_{n}", [4, 128, SK // 128, 128])
        s["QTB"] = dscr(f"qtb_{n}", [4, 128, SQ])
        s["KTB"] = dscr(f"ktb_{n}", [2, 128, SK])
        s["VB"] = dscr(f"vb_{n}", [2, 128, SK // 128, 65])
        s["MXA"] = dscr(f"mxa_{n}", [4, 128, SQ])
        s["MXB"] = dscr(f"mxb_{n}", [8, 64, SQ])
    w_in = din("w_in", [D, 2304])
    w_out = din("w_out", [D, D])
    w_ff1 = din("w_ff1", [D, DFF])
    w_ff2 = din("w_ff2", [DFF, D])
    w_gate = din("w_gate", [D, D])
    w_proj = din("w_proj", [PLE, D])
    gvec = din("gvec", [4, D])
    lamv = din("lamv", [4, 64])
    gsub = din("gsub", [128, 1])
    gqk = din("gqk", [1, 640])
    ident_d = din("ident", [128, 128])
    BL_d = din("alibi_bl", [128, 4 * 64])
    BR_d = din("alibi_br", [128, 4 * 64])
    FL_d = din("alibi_fl", [128, 4 * 512])
    FR_d = din("alibi_fr", [128, 4 * 512])
    DT_d = din("alibi_dt", [128, 4 * 512])
    THI_d = din("alibi_hi", [128, 4 * 512])
    TLO_d = din("alibi_lo", [128, 4 * 512])
    W1S = dscr("w1s", [4, 128, 8, 1024])
    W2S = dscr("w2s", [4, 128, 8, 1024])

    PS = [nc.alloc_psum_tensor(f"ps{i}", [128, 1024], F32).ap() for i in range(4)]
    RB = [Res(f"bank{i}", excl=True) for i in range(8)]

    def bank(i):
        return PS[i // 2][:, (i % 2) * 512:(i % 2 + 1) * 512]

    def rs_(ts):
        return [t.r if isinstance(t, T) else t for t in ts]

    def dma(eng, key, out, in_, reads=(), writes=()):
        P.add(eng, lambda e: e.dma_start(out=out, in_=in_), rs_(reads), rs_(writes), dma=key)

    def mm(out, lhsT, rhs, start, stop, reads, writes):
        P.add("pe", lambda e: e.matmul(out, lhsT=lhsT, rhs=rhs, start=start, stop=stop),
              rs_(reads), rs_(writes))

    def tr(out, in_, ident, reads, writes):
        P.add("pe", lambda e: e.transpose(out=out, in_=in_, identity=ident), rs_(reads), rs_(writes))

    def act(out, in_, func, reads, writes, bias=None, scale=1.0, accum=None):
        def fn(e):
            kw = dict(out=out, in_=in_, func=func, scale=scale)
            if bias is not None:
                kw["bias"] = bias
            if accum is not None:
                kw["accum_out"] = accum
            return e.activation(**kw)
        P.add("act", fn, rs_(reads), rs_(writes))

    def tt(eng, out, in0, in1, op, reads, writes):
        P.add(eng, lambda e: e.tensor_tensor(out=out, in0=in0, in1=in1, op=op), rs_(reads), rs_(writes))

    def stt(eng, out, in0, scalar, in1, op0, op1, reads, writes):
        P.add(eng, lambda e: e.scalar_tensor_tensor(out=out, in0=in0, scalar=scalar, in1=in1,
                                                    op0=op0, op1=op1), rs_(reads), rs_(writes))

    def ts(eng, out, in0, s1, s2, op0, op1, reads, writes):
        if op1 is None:
            P.add(eng, lambda e: e.tensor_scalar(out=out, in0=in0, scalar1=s1, scalar2=None, op0=op0),
                  rs_(reads), rs_(writes))
        else:
            P.add(eng, lambda e: e.tensor_scalar(out=out, in0=in0, scalar1=s1, scalar2=s2, op0=op0, op1=op1),
                  rs_(reads), rs_(writes))

    def cp(eng, out, in_, reads, writes):
        if eng == "act":
            P.add("act", lambda e: e.copy(out=out, in_=in_), rs_(reads), rs_(writes))
        else:
            P.add(eng, lambda e: e.tensor_copy(out=out, in_=in_), rs_(reads), rs_(writes))

    def memset(eng, ap, val, writes):
        P.add(eng, lambda e: e.memset(ap, val), (), rs_(writes))

    def recip(out, in_, reads, writes):
        P.add("dve", lambda e: e.reciprocal(out=out, in_=in_), rs_(reads), rs_(writes))

    ident_f = A([128, 128], F32)
    ident = A([128, 128], BF16)
    eps6 = A([128, 1], F32)
    eps5 = A([128, 1], F32)
    zero1 = A([128, 1], F32)
    ones_f = A([128, 128], F32)
    onesdiv = A([128, 128], F32)
    e0 = A([128, 33], BF16)
    e32 = A([128, 33], BF16)
    lamt = A([128, 4, 64], F32)
    neglam = A([128, 1], F32)
    gsub08 = A([128, 1], F32)
    lamtmp = A([128, 2, 64], F32)
    lamsum = A([128, 2], F32)
    ssA = A([128, 1], F32)
    rstdA = A([128, 1], F32)

    dma("sp", "c1", ident_f.ap, ident_d, writes=[ident_f])
    cp("dve", ident.ap, ident_f.ap, [ident_f], [ident])
    memset("pool", eps6.ap, 1e-6, [eps6])
    memset("pool", eps5.ap, 1e-5, [eps5])
    memset("pool", zero1.ap, 0.0, [zero1])
    memset("pool", ones_f.ap, 1.0, [ones_f])
    memset("pool", onesdiv.ap, 1.0 / 128.0, [onesdiv])
    memset("pool", e0.ap, 0.0, [e0])
    memset("pool", e0.ap[:, 0:1], 1.0, [e0])
    memset("pool", e32.ap, 0.0, [e32])
    memset("pool", e32.ap[:, 32:33], 1.0, [e32])
    for i in range(4):
        dma("sp", "clam", lamt.ap[:, i, :], lamv[i:i + 1, :].partition_broadcast(128), writes=[lamt])
    tt("dve", lamtmp.ap[:, 0, :], lamt.ap[:, 0, :], lamt.ap[:, 1, :], ALU.mult, [lamt], [lamtmp])
    tt("dve", lamtmp.ap[:, 1, :], lamt.ap[:, 2, :], lamt.ap[:, 3, :], ALU.mult, [lamt], [lamtmp])
    P.add("dve", lambda e: e.tensor_reduce(out=lamsum.ap, in_=lamtmp.ap, axis=AX.X, op=ALU.add),
          [lamtmp.r], [lamsum.r])
    act(lamsum.ap, lamsum.ap, AF.Exp, [lamsum], [lamsum])
    tt("dve", neglam.ap, lamsum.ap[:, 1:2], lamsum.ap[:, 0:1], ALU.subtract, [lamsum], [neglam])
    ts("dve", neglam.ap, neglam.ap, -LAM_INIT, None, ALU.add, None, [neglam], [neglam])
    dma("sp", "c2", gsub08.ap, gsub, writes=[gsub08])
    ts("dve", gsub08.ap, gsub08.ap, 1.0 - LAM_INIT, None, ALU.mult, None, [gsub08], [gsub08])

    e0f = A([128, 33], F32)
    e32f = A([128, 33], F32)
    memset("pool", e0f.ap, 0.0, [e0f])
    memset("pool", e0f.ap[:, 0:1], 1.0, [e0f])
    memset("pool", e32f.ap, 0.0, [e32f])
    memset("pool", e32f.ap[:, 32:33], 1.0, [e32f])
    wstg_f = [A([128, 1024], F32) for _ in range(2)]
    wstg_b = [A([128, 1024], BF16) for _ in range(2)]
    bg_tasks = []

    def _mk_bg(kk, src, dst):
        def task():
            sf, sbb = wstg_f[kk % 2], wstg_b[kk % 2]
            dma("sp", f"bwl{kk % 2}", sf.ap, src, writes=[sf])
            cp("pool", sbb.ap, sf.ap, [sf], [sbb])
            dma("sp", f"bws{kk % 2}", dst, sbb.ap, reads=[sbb])
        return task

    _kk = 0
    for q in range(4):
        for c in range(8):
            bg_tasks.append(_mk_bg(_kk, w_ff1[c * 128:(c + 1) * 128, q * 1024:(q + 1) * 1024], W1S[q, :, c, :]))
            _kk += 1
    for q in range(4):
        for c in range(8):
            r0 = q * 1024 + c * 128
            bg_tasks.append(_mk_bg(_kk, w_ff2[r0:r0 + 128, :], W2S[q, :, c, :]))
            _kk += 1

    base_mark = A.mark()

    def rms_rstd(x_ap, xr, junk, nfree, eps, ss, rstd):
        act(junk.ap, x_ap, AF.Square, [xr], [junk, ss], accum=ss.ap)
        act(rstd.ap, ss.ap, AF.Ln, [ss, eps], [rstd], bias=eps.ap, scale=1.0 / nfree)
        act(rstd.ap, rstd.ap, AF.Exp, [rstd], [rstd], scale=-0.5)

    stg_f = [A([128, 1152], F32) for _ in range(2)]
    win_b = A([128, 8, 2304], BF16)
    gmix = A([128, D], F32)
    dma("sp", "c3", gmix.ap, gvec[0:1, :].partition_broadcast(128), writes=[gmix])
    k = 0
    for c in range(8):
        for hf in range(2):
            sf = stg_f[k % 2]
            dma("sp", f"wld{k % 2}", sf.ap, w_in[c * 128:(c + 1) * 128, hf * 1152:(hf + 1) * 1152], writes=[sf])
            cp("pool", win_b.ap[:, c, hf * 1152:(hf + 1) * 1152], sf.ap, [sf], [win_b])
            k += 1

    a_mark = A.mark()
    NX = 4
    xt = [A([128, D], F32) for _ in range(NX)]
    junkb = A([128, D], BF16)
    ssl = [A([128, 1], F32) for _ in range(NX)]
    rstdl = [A([128, 1], F32) for _ in range(NX)]
    ntl = [A([128, D], BF16) for _ in range(4)]
    nT4 = [A([128, 8, 512], BF16) for _ in range(2)]
    stA = [A([128, 4, 512], BF16) for _ in range(4)]
    vst = [A([128, 4, 512], BF16) for _ in range(2)]
    gqk_t = A([128, 10, 64], F32)
    qkT = [A([128, 6, 512], BF16) for _ in range(2)]
    vbst = [A([128, 4, 2, 65], BF16) for _ in range(2)]
    dma("sp", "c4", gqk_t.ap.rearrange("p h d -> p (h d)"), gqk[0:1, :].partition_broadcast(128), writes=[gqk_t])
    for v in vbst:
        memset("pool", v.ap, 1.0, [v])

    groups = []
    for s in seqs:
        for g in range(s["SK"] // 512):
            groups.append((s, g))
    NSET = 4
    sq = [A([128, 640], F32) for _ in range(NSET)]
    ssq = [A([128, 10], F32) for _ in range(NSET)]
    rq = [A([128, 10], F32) for _ in range(NSET)]
    yb = [A([128, 10, 64], F32) for _ in range(NSET)]
    t1 = [A([128, 10, 64], F32) for _ in range(NSET)]
    t2 = [A([128, 10, 64], F32) for _ in range(NSET)]
    qkb = [A([128, 12, 64], BF16) for _ in range(NSET)]
    ropec = [A([128, 64], F32) for _ in range(NSET)]
    ropes = [A([128, 64], F32) for _ in range(NSET)]
    cnt = dict(tile=0, stA=0, pb=0)
    def a_bank():
        b = cnt["pb"] % 6
        cnt["pb"] += 1
        return b

    def Nelem(gi):
        s, g = groups[gi]
        for t in range(4):
            tok0 = g * 512 + t * 128
            k_ = (gi * 4 + t)
            X = xt[k_ % NX]
            ss_, rstd_ = ssl[k_ % NX], rstdl[k_ % NX]
            dma("sp", f"x{k_ % NX}", X.ap, s["x"][tok0:tok0 + 128, :], writes=[X])
            rms_rstd(X.ap, X, junkb, D, eps6, ss_, rstd_)
            stt("dve", ntl[t].ap, X.ap, rstd_.ap, gmix.ap, ALU.mult, ALU.mult, [X, rstd_, gmix], [ntl[t]])

    def Ntr(gi):
        nT = nT4[gi % 2]
        for t in range(4):
            tb = 6 + (t % 2)
            psT = bank(tb).bitcast(BF16)
            for c in range(8):
                tr(psT[:, c * 128:(c + 1) * 128], ntl[t].ap[:, c * 128:(c + 1) * 128], ident.ap,
                   [ntl[t], ident], [RB[tb]])
            cp("dve", nT.ap[:, :, t * 128:(t + 1) * 128],
               psT[:, 0:1024].rearrange("p (c k) -> p c k", k=128), [RB[tb]], [nT])

    def Bm(gi):
        s, g = groups[gi]
        own = (g * 512) < s["SQ"]
        nT = nT4[gi % 2]
        for t in range(4):
            bq, bkv, ko = t, 4 + t // 2, (t % 2) * 256
            if own:
                for c in range(8):
                    mm(bank(bq), nT.ap[:, c, t * 128:(t + 1) * 128], win_b.ap[:, c, 1536:2048], c == 0, c == 7,
                       [win_b, nT], [RB[bq]])
            for c in range(8):
                mm(bank(bkv)[:, ko:ko + 256], nT.ap[:, c, t * 128:(t + 1) * 128], win_b.ap[:, c, 2048:2304],
                   c == 0, c == 7, [win_b, nT], [RB[bkv]])

    def Bc(gi):
        s, g = groups[gi]
        own = (g * 512) < s["SQ"]
        VB_ = vbst[gi % 2]
        h0 = 0 if own else 8
        nh = 10 - h0
        T4 = range(4)

        def bk(t):
            return t, 4 + t // 2, (t % 2) * 256

        for t in T4:
            tok0 = g * 512 + t * 128
            dma("sp", f"rope{t}", ropec[t].ap, s["rc"][tok0:tok0 + 128, :], writes=[ropec[t]])
            dma("sp", f"rope{t}", ropes[t].ap, s["rs"][tok0:tok0 + 128, :], writes=[ropes[t]])
        for t in T4:
            bq, bkv, ko = bk(t)
            if own:
                act(sq[t].ap[:, 0:512], bank(bq), AF.Square, [RB[bq]], [sq[t]])
            act(sq[t].ap[:, 512:640], bank(bkv)[:, ko:ko + 128], AF.Square, [RB[bkv]], [sq[t]])
        for t in T4:
            P.add("dve", lambda e, o=ssq[t].ap[:, h0:10], i=sq[t].ap[:, h0 * 64:640].rearrange("p (h d) -> p h d", d=64):
                  e.tensor_reduce(out=o, in_=i, axis=AX.X, op=ALU.add), [sq[t].r], [ssq[t].r])
        for t in T4:
            act(rq[t].ap[:, h0:10], ssq[t].ap[:, h0:10], AF.Ln, [ssq[t], eps6], [rq[t]], bias=eps6.ap, scale=1.0 / 64)
        for t in T4:
            act(rq[t].ap[:, h0:10], rq[t].ap[:, h0:10], AF.Exp, [rq[t]], [rq[t]], scale=-0.5)
        for t in T4:
            bq, bkv, ko = bk(t)
            cp("act", VB_.ap[:, t, :, 0:64], bank(bkv)[:, ko + 128:ko + 256].rearrange("p (h d) -> p h d", d=64),
               [RB[bkv]], [VB_])
        for t in T4:
            bq, bkv, ko = bk(t)
            Y, RQ = yb[t], rq[t]
            if own:
                tt("dve", Y.ap[:, 0:8, :], bank(bq).rearrange("p (h d) -> p h d", d=64),
                   RQ.ap[:, 0:8].unsqueeze(2).broadcast_to([128, 8, 64]), ALU.mult, [RB[bq], RQ], [Y])
            tt("dve", Y.ap[:, 8:10, :], bank(bkv)[:, ko:ko + 128].rearrange("p (h d) -> p h d", d=64),
               RQ.ap[:, 8:10].unsqueeze(2).broadcast_to([128, 2, 64]), ALU.mult, [RB[bkv], RQ], [Y])
        for t in T4:
            ysl = yb[t].ap[:, h0:10, :]
            tt("dve", ysl, ysl, gqk_t.ap[:, h0:10, :], ALU.mult, [yb[t], gqk_t], [yb[t]])
        for t in T4:
            ysl = yb[t].ap[:, h0:10, :]
            y5 = ysl.rearrange("p h (a b d) -> p h a b d", a=2, b=2)
            t5 = t2[t].ap[:, h0:10, :].rearrange("p h (a b d) -> p h a b d", a=2, b=2)
            s5 = ropes[t].ap.rearrange("p (a b d) -> p a b d", a=2, b=2)
            for bb in range(2):
                tt("pool", t5[:, :, :, bb, :], y5[:, :, :, 1 - bb, :],
                   s5[:, :, bb, :].unsqueeze(1).broadcast_to([128, nh, 2, 16]), ALU.mult, [yb[t], ropes[t]], [t2[t]])
        for t in T4:
            ysl = yb[t].ap[:, h0:10, :]
            tt("dve", t1[t].ap[:, h0:10, :], ysl, ropec[t].ap.unsqueeze(1).broadcast_to([128, nh, 64]), ALU.mult,
               [yb[t], ropec[t]], [t1[t]])
        for t in T4:
            T1, T2, QKB = t1[t], t2[t], qkb[t]
            if own:
                tt("pool", QKB.ap[:, 0:8, :], T1.ap[:, 0:8, :], T2.ap[:, 0:8, :], ALU.add, [T1, T2], [QKB])
            for dup in range(2):
                tt("pool", QKB.ap[:, 8:12, :].rearrange("p (n u) d -> p n u d", u=2)[:, :, dup, :],
                   T1.ap[:, 8:10, :], T2.ap[:, 8:10, :], ALU.add, [T1, T2], [QKB])

    def Bt(gi):
        s, g = groups[gi]
        own = (g * 512) < s["SQ"]
        G = gi % 2
        QT_, VB_ = qkT[G], vbst[G]
        j0 = 0 if own else 4
        for t in range(4):
            QKB = qkb[t]
            tb = 6 + (t % 2)
            psT = bank(tb).bitcast(BF16)
            for j in range(j0, 6):
                tr(psT[:, j * 128:(j + 1) * 128], QKB.ap[:, 2 * j:2 * j + 2, :].rearrange("p h d -> p (h d)"),
                   ident.ap, [QKB, ident], [RB[tb]])
            cp("dve", QT_.ap[:, j0:6, t * 128:(t + 1) * 128],
               psT[:, j0 * 128:768].rearrange("p (j k) -> p j k", k=128), [RB[tb]], [QT_])
        if own:
            dma("pool", f"qkT{G}", s["QTB"][:, :, g * 512:(g + 1) * 512].rearrange("h p t -> p h t"),
                QT_.ap[:, 0:4, :], reads=[QT_])
        dma("pool", f"qkT{G}", s["KTB"][:, :, g * 512:(g + 1) * 512].rearrange("h p t -> p h t"),
            QT_.ap[:, 4:6, :], reads=[QT_])
        for n in range(2):
            dma("pool", f"vbst{G}", s["VB"][n, :, g * 4:(g + 1) * 4, :], VB_.ap[:, :, n, :], reads=[VB_])

    def Pa(gi):
        s, g = groups[gi]
        own = (g * 512) < s["SQ"]
        G = gi % 2
        nT = nT4[G]
        jobs = []
        if own:
            jobs += [("QTA", h, h * 128) for h in range(4)]
        jobs += [("KTA", h, 512 + h * 128) for h in range(4)]
        for (dst, h, col) in jobs:
            b = a_bank()
            for c in range(8):
                mm(bank(b), win_b.ap[:, c, col:col + 128], nT.ap[:, c, :], c == 0, c == 7,
                   [win_b, nT], [RB[b]])
            k_ = 2 * G + (0 if dst == "QTA" else 1)
            S_ = stA[k_]
            cp("act", S_.ap[:, h, :], bank(b), [RB[b]], [S_])
            if h == 3:
                dma("pool", f"stA{k_}", s[dst][:, :, g * 512:(g + 1) * 512].rearrange("h p t -> p h t"), S_.ap,
                    reads=[S_])
        V = vst[G]
        for t in range(4):
            b = a_bank()
            for c in range(8):
                mm(bank(b), nT.ap[:, c, t * 128:(t + 1) * 128], win_b.ap[:, c, 1024:1536], c == 0, c == 7,
                   [win_b, nT], [RB[b]])
            cp("act", V.ap[:, t, :], bank(b), [RB[b]], [V])
        for h in range(4):
            dma("pool", f"vst{G}", s["VA"][h, :, g * 4:(g + 1) * 4, :], V.ap[:, :, h * 128:(h + 1) * 128],
                reads=[V])

    Nelem(0)
    Ntr(0)
    for gi in range(len(groups)):
        Bm(gi)
        if gi + 1 < len(groups):
            Nelem(gi + 1)
            Ntr(gi + 1)
        Bc(gi)
        Pa(gi)
        Bt(gi)

    P.barrier()
    A.release(base_mark)
    if "A" == debug:
        P.emit()
        return nc

    b_mark = A.mark()
    KTm, QTm, VM = SKP * 2, SQP * 2, (SKP // 128) * 128 * 2
    slots = []
    for i in range(2):
        o_k = (A.mark() + 31) // 32 * 32
        A.off = o_k + KTm
        o_q = A.off
        A.off = o_q + QTm
        o_v = A.off
        A.off = o_v + VM
        slots.append(dict(ok=o_k, oq=o_q, ov=o_v, rk=Res(f"slk{i}"), rq=Res(f"slq{i}"), rv=Res(f"slv{i}")))
    BL = A([128, 4, 64], F32)
    BR = A([128, 4, 64], F32)
    FL = A([128, 4, 512], F32)
    FR = A([128, 4, 512], F32)
    DT = A([128, 4, 512], F32)
    dma("sp", "c5", BL.ap.rearrange("p h m -> p (h m)"), BL_d, writes=[BL])
    dma("sp", "c6", BR.ap.rearrange("p h m -> p (h m)"), BR_d, writes=[BR])
    dma("sp", "c7", FL.ap.rearrange("p h m -> p (h m)"), FL_d, writes=[FL])
    dma("sp", "c8", FR.ap.rearrange("p h m -> p (h m)"), FR_d, writes=[FR])
    PT = [A([128, 1024], BF16) for _ in range(3)]
    thi_b = A([128, 4, 512], BF16)
    tlo_b = A([128, 4, 512], BF16)
    cI = [A([128, 128], BF16) for _ in range(4)]
    for (dst_, src_, key_) in ((thi_b, THI_d, "c20"), (tlo_b, TLO_d, "c21")):
        dma("sp", key_, DT.ap.rearrange("p h m -> p (h m)"), src_, writes=[DT])
        cp("pool", dst_.ap.rearrange("p h m -> p (h m)"), DT.ap.rearrange("p h m -> p (h m)"), [DT], [dst_])
    for h_ in range(4):
        ts("dve", cI[h_].ap, ident_f.ap, 8.0 * SLOPES[h_], None, ALU.mult, None, [ident_f], [cI[h_]])
    acc0 = [A([128, 512], F32) for _ in range(2)]
    acc1 = [A([128, 512], F32) for _ in range(2)]
    accL = [A([33, 512], F32) for _ in range(2)]
    raccs = [A([128, 512], F32) for _ in range(2)]
    tmpA = [A([128, 512], F32) for _ in range(2)]
    tmpL = A([33, 512], F32)
    rL = A([33, 512], F32)
    T0 = A([128, 512], F32)
    T1b = A([128, 512], F32)
    Ot = A([128, 512], F32)
    sqO = A([128, 512], F32)
    rsO = A([128, 512], F32)
    mxs = [A([128, 512], BF16) for _ in range(2)]
    ob0 = [T(acc0[i].ap[0:65, :], acc0[i].r) for i in range(2)]
    ob1 = [T(acc1[i].ap[0:65, :], acc1[i].r) for i in range(2)]
    mxbs = [A([64, 512], BF16) for _ in range(2)]

    ST = [PS[0], PS[1]]
    RST = [[RB[0], RB[1]], [RB[2], RB[3]]]
    OA = [bank(4), bank(5)]
    ROA = [RB[4], RB[5]]
    LA = bank(6)
    RLA = RB[6]
    MISC = bank(7)
    RM = RB[7]

    units = []
    hctr = 0
    segn_ctr = [0]
    for s in seqs:
        SK, SQ = s["SK"], s["SQ"]
        NJ, NI = SK // 128, SQ // 512
        for h in range(4):
            ctx = dict(kind="A", s=s, h=h, slot=slots[hctr % 2], idx=hctr)
            hctr += 1
            first_u = True
            for I in range(NI):
                segs = [("D", [4 * I + d for d in range(4)])]
                if I > 0:
                    segs.append(("L", list(range(0, 4 * I))))
                if 4 * I + 4 < NJ:
                    segs.append(("R", list(range(4 * I + 4, NJ))))
                for si, (sk, js) in enumerate(segs):
                    segn_ctr[0] += 1
                    for ji, J in enumerate(js):
                        units.append(dict(ctx=ctx, segn=segn_ctr[0], I=I, seg=sk, J=J, first=(ji == 0), last=(ji == len(js) - 1),
                                          seg_first=(si == 0), seg_last=(si == len(segs) - 1), load=first_u))
                        first_u = False
        for n in range(2):
            for pr in range(2):
                ctx = dict(kind="B", s=s, n=n, pr=pr, slot=slots[hctr % 2], idx=hctr)
                hctr += 1
                first_u = True
                for I in range(NI):
                    for J in range(NJ):
                        units.append(dict(ctx=ctx, I=I, seg="G", J=J, first=(J == 0), last=(J == NJ - 1),
                                          seg_first=True, seg_last=True, load=first_u))
                        first_u = False

    def head_views(ctx):
        if "kt" in ctx:
            return
        s, sl = ctx["s"], ctx["slot"]
        SK, SQ = s["SK"], s["SQ"]
        ctx["kt"] = A.raw([128, SK], BF16, sl["ok"])
        ctx["qt"] = A.raw([128, SQ], BF16, sl["oq"])
        if ctx["kind"] == "A":
            ctx["v"] = A.raw([128, SK // 128, 128], BF16, sl["ov"])
        else:
            ctx["v"] = A.raw([128, SK // 128, 65], BF16, sl["ov"])

    def load_head(ctx):
        head_views(ctx)
        s, sl = ctx["s"], ctx["slot"]
        key = f"hd{ctx['idx'] % 2}"
        if ctx["kind"] == "A":
            h = ctx["h"]
            dma("sp", key, ctx["kt"], s["KTA"][h], writes=[sl["rk"]])
            dma("sp", key, ctx["qt"], s["QTA"][h], writes=[sl["rq"]])
            dma("sp", key, ctx["v"], s["VA"][h], writes=[sl["rv"]])
        else:
            n, pr = ctx["n"], ctx["pr"]
            dma("sp", key, ctx["kt"], s["KTB"][n], writes=[sl["rk"]])
            dma("sp", key, ctx["qt"], s["QTB"][2 * n + pr], writes=[sl["rq"]])
            dma("sp", key, ctx["v"], s["VB"][n], writes=[sl["rv"]])

    def emit_qk(u, idx):
        ctx = u["ctx"]
        sl = ctx["slot"]
        b = idx % 2
        I, J = u["I"], u["J"]
        kt, qt = ctx["kt"], ctx["qt"]
        diag = (u["seg"] == "D")
        mm(ST[b][:, 0:512], kt[0:64, J * 128:(J + 1) * 128], qt[0:64, I * 512:(I + 1) * 512], True, not diag,
           [sl["rk"], sl["rq"]], [RST[b][0]])
        mm(ST[b][:, 512:1024], kt[64:128, J * 128:(J + 1) * 128], qt[64:128, I * 512:(I + 1) * 512], True, not diag,
           [sl["rk"], sl["rq"]], [RST[b][1]])
        if diag:
            h = ctx["h"]
            d = J - 4 * I
            for c in range(2):
                o = ST[b][:, c * 512:(c + 1) * 512]
                mm(o, cI[h].ap, thi_b.ap[:, d, :], False, False, [cI[h], thi_b], [RST[b][c]])
                mm(o, cI[h].ap, tlo_b.ap[:, d, :], False, True, [cI[h], tlo_b], [RST[b][c]])

    def emit_exp(u, idx):
        ctx = u["ctx"]
        b = idx % 2
        pt = PT[idx % 3]
        I, J = u["I"], u["J"]
        if u["seg"] == "L":
            bias, br = BL.ap[:, ctx["h"], 4 * I - J:4 * I - J + 1], BL
        elif u["seg"] == "R":
            m = J - 4 * I - 4
            bias, br = BR.ap[:, ctx["h"], m:m + 1], BR
        else:
            bias, br = zero1.ap, zero1
        act(pt.ap, ST[b], AF.Exp, RST[b] + [br], [pt], bias=bias, scale=0.125)
        if ctx["kind"] == "A":
            racc = raccs[u["segn"] % 2]
            if u["first"]:
                cp("dve", racc.ap[:, 0:512], pt.ap[:, 0:512], [pt], [racc])
            else:
                tt("dve", racc.ap[:, 0:512], racc.ap[:, 0:512], pt.ap[:, 0:512], ALU.add, [racc, pt], [racc])

    def emit_pv(u, idx):
        ctx = u["ctx"]
        sl = ctx["slot"]
        pt = PT[idx % 3]
        J = u["J"]
        v = ctx["v"]
        if ctx["kind"] == "A":
            for c in range(2):
                mm(OA[c], v[:, J, :], pt.ap[:, c * 512:(c + 1) * 512], u["first"], u["last"],
                   [sl["rv"], pt], [ROA[c]])
            mm(LA[0:33, :], e32.ap, pt.ap[:, 512:1024], u["first"], False, [e32, pt], [RLA])
            if u["last"]:
                racc = raccs[u["segn"] % 2]
                mm(LA[0:33, :], e0f.ap, racc.ap[:, 0:512], False, True, [e0f, racc], [RLA])
        else:
            for c in range(2):
                mm(OA[c][0:65, :], v[:, J, :], pt.ap[:, c * 512:(c + 1) * 512], u["first"], u["last"],
                   [sl["rv"], pt], [ROA[c]])

    deferred = []
    cur = [0]

    def defer(k, fn):
        deferred.append((cur[0] + k, fn))

    def run_deferred(force=False):
        keep = []
        for (due, fn) in deferred:
            if force or due <= cur[0]:
                fn()
            else:
                keep.append((due, fn))
        deferred[:] = keep

    qb_ctr = [0]

    def seg_end_A(u):
        ctx = u["ctx"]
        h = ctx["h"]
        sk = u["seg"]
        pq = qb_ctr[0] % 2
        a0, a1, aL = acc0[pq], acc1[pq], accL[pq]
        if sk == "D":
            cp("dve", a0.ap, OA[0], [ROA[0]], [a0])
            cp("act", a1.ap, OA[1], [ROA[1]], [a1])
            cp("dve", aL.ap, LA[0:33, :], [RLA], [aL])
        else:
            Ft = FL if sk == "L" else FR
            tt("dve", tmpA[0].ap, OA[0], Ft.ap[:, h, :], ALU.mult, [ROA[0], Ft], [tmpA[0]])
            cp("act", tmpA[1].ap, OA[1], [ROA[1]], [tmpA[1]])
            tt("dve", tmpL.ap, LA[0:33, :], Ft.ap[0:33, h, :], ALU.mult, [RLA, Ft], [tmpL])
            tt("pool", a0.ap, a0.ap, tmpA[0].ap, ALU.add, [a0, tmpA[0]], [a0])
            tt("pool", tmpA[1].ap, tmpA[1].ap, Ft.ap[:, h, :], ALU.mult, [tmpA[1], Ft], [tmpA[1]])
            tt("pool", a1.ap, a1.ap, tmpA[1].ap, ALU.add, [a1, tmpA[1]], [a1])
            tt("pool", aL.ap, aL.ap, tmpL.ap, ALU.add, [aL, tmpL], [aL])

    def epilogue_A(u):
        ctx = u["ctx"]
        h, I, s = ctx["h"], u["I"], ctx["s"]
        run_deferred(force=True)
        pq = qb_ctr[0] % 2
        qb_ctr[0] += 1
        a0, a1, aL = acc0[pq], acc1[pq], accL[pq]
        mx = mxs[pq]

        def s0():
            act(rL.ap, aL.ap, AF.Ln, [aL], [rL])
            act(rL.ap, rL.ap, AF.Exp, [rL], [rL], scale=-1.0)

        def s1():
            mm(MISC, ones_f.ap[0:1, :], rL.ap[0:1, :], True, True, [ones_f, rL], [RM])
            tt("dve", T0.ap, a0.ap, MISC, ALU.mult, [a0, RM], [T0])

        def s2():
            mm(MISC, ones_f.ap[32:33, :], rL.ap[32:33, :], True, True, [ones_f, rL], [RM])
            tt("dve", T1b.ap, a1.ap, MISC, ALU.mult, [a1, RM], [T1b])
            stt("dve", Ot.ap, T1b.ap, neglam.ap, T0.ap, ALU.mult, ALU.add, [T1b, T0, neglam], [Ot])

        def s3():
            act(sqO.ap, Ot.ap, AF.Square, [Ot], [sqO])

        def s4():
            mm(MISC, onesdiv.ap, sqO.ap, True, True, [onesdiv, sqO], [RM])

        def s5():
            act(rsO.ap, MISC, AF.Ln, [RM, eps5], [rsO], bias=eps5.ap, scale=1.0)
            act(rsO.ap, rsO.ap, AF.Exp, [rsO], [rsO], scale=-0.5)
            stt("dve", mx.ap, Ot.ap, gsub08.ap, rsO.ap, ALU.mult, ALU.mult, [Ot, gsub08, rsO], [mx])
            dma("pool", f"mx{pq}", s["MXA"][h, :, I * 512:(I + 1) * 512], mx.ap, reads=[mx])

        defer(4, s0)
        defer(8, s1)
        defer(10, s2)
        defer(12, s3)
        defer(14, s4)
        defer(16, s5)

    def epilogue_B(u):
        ctx = u["ctx"]
        n, pr, I, s = ctx["n"], ctx["pr"], u["I"], ctx["s"]
        run_deferred(force=True)
        pq = qb_ctr[0] % 2
        qb_ctr[0] += 1
        obs = [ob0[pq], ob1[pq]]
        cp("act", obs[0].ap, OA[0][0:65, :], [ROA[0]], [obs[0]])
        cp("dve", obs[1].ap, OA[1][0:65, :], [ROA[1]], [obs[1]])
        for c in range(2):
            head = 4 * n + 2 * pr + c
            ob = obs[c]
            mxb = mxbs[c]

            def s0(ob=ob):
                recip(ob.ap[64:65, :], ob.ap[64:65, :], [ob], [ob])

            def s1(ob=ob, mxb=mxb, head=head, c=c):
                mm(MISC[0:64, :], ones_f.ap[64:65, 0:64], ob.ap[64:65, :], True, True, [ones_f, ob], [RM])
                tt("dve", mxb.ap, ob.ap[0:64, :], MISC[0:64, :], ALU.mult, [ob, RM], [mxb])
                dma("pool", f"mxb{c}", s["MXB"][head, :, I * 512:(I + 1) * 512], mxb.ap, reads=[mxb])

            defer(2 + 4 * c, s0)
            defer(6 + 4 * c, s1)

    NU = len(units)
    ctxs = []
    for u in units:
        if u["load"]:
            ctxs.append(u["ctx"])
    bg_every = max(1, NU // (len(bg_tasks) + 1))
    LAG = 2
    for idx in range(NU + LAG):
        cur[0] = idx
        run_deferred()
        if idx % bg_every == bg_every - 1 and bg_tasks:
            bg_tasks.pop(0)()
        if idx < NU:
            u = units[idx]
            if u["load"] and u["ctx"]["idx"] == 0:
                load_head(ctxs[0])
            emit_qk(u, idx)
            emit_exp(u, idx)
        if idx >= LAG:
            u = units[idx - LAG]
            emit_pv(u, idx - LAG)
            if u["last"]:
                if u["ctx"]["kind"] == "A":
                    seg_end_A(u)
                    if u["seg_last"]:
                        epilogue_A(u)
                else:
                    epilogue_B(u)
        j = idx - (LAG - 1)
        if 0 <= j < NU and units[j]["load"]:
            ci = units[j]["ctx"]["idx"]
            if ci + 1 < len(ctxs):
                load_head(ctxs[ci + 1])

    run_deferred(force=True)
    while bg_tasks:
        bg_tasks.pop(0)()
    P.barrier()
    A.release(base_mark)
    if "B" == debug:
        P.emit()
        return nc

    wout_a = A([128, 8, D], BF16)
    wgate = A([128, 8, D], BF16)
    wproj = A([128, 2, D], BF16)
    gmlp = A([128, D], F32)
    gple = A([128, D], F32)
    gfin = A([128, D], F32)
    dma("sp", "c10", gmlp.ap, gvec[1:2, :].partition_broadcast(128), writes=[gmlp])
    dma("sp", "c11", gple.ap, gvec[2:3, :].partition_broadcast(128), writes=[gple])
    dma("sp", "c12", gfin.ap, gvec[3:4, :].partition_broadcast(128), writes=[gfin])
    cstg = wstg_f
    k = 0
    for c in range(8):
        sf = cstg[k % 2]
        dma("sp", f"cw{k % 2}", sf.ap, w_out[c * 128:(c + 1) * 128, :], writes=[sf])
        cp("pool", wout_a.ap[:, c, :], sf.ap, [sf], [wout_a])
        k += 1
    for c in range(8):
        sf = cstg[k % 2]
        dma("sp", f"cw{k % 2}", sf.ap, w_gate[c * 128:(c + 1) * 128, :], writes=[sf])
        cp("pool", wgate.ap[:, c, :], sf.ap, [sf], [wgate])
        k += 1
    for c in range(2):
        sf = cstg[k % 2]
        dma("sp", f"cw{k % 2}", sf.ap, w_proj[c * 128:(c + 1) * 128, :], writes=[sf])
        cp("pool", wproj.ap[:, c, :], sf.ap, [sf], [wproj])
        k += 1

    w1q = [A([128, 8, 512], BF16) for _ in range(2)]
    w2q = [A([128, 4, 1024], BF16) for _ in range(2)]
    xh = [[A([128, D], F32) for _ in range(4)] for _ in range(2)]
    mxa = [A([128, 4, 512], BF16) for _ in range(2)]
    mxb_ = [A([128, 4, 512], BF16) for _ in range(2)]
    plt = A([128, 4, PLE], F32)
    plb = A([128, PLE], BF16)
    pT = A([128, 2, 512], BF16)
    n2 = wstg_b
    n2T = A([128, 8, 512], BF16)
    rl_ = [A([128, 512], F32) for _ in range(2)]
    a2T = [A([128, 4, 512], BF16) for _ in range(2)]
    junkc = A([128, D], BF16)
    ssc = [A([128, 1], F32) for _ in range(2)]
    rstdc = [A([128, 1], F32) for _ in range(2)]
    egate = [A([128, D], F32) for _ in range(4)]
    yout = cstg

    PA = [PS[0], PS[1]]
    RPA = [[RB[0], RB[1]], [RB[2], RB[3]]]
    pa_ctr = 0
    pf_ctr = 0
    wq_ctr = 0
    nrm_ctr = 0
    gctr = 0

    def norm_T(Xs, gt, dstT):
        nonlocal nrm_ctr
        for t in range(4):
            i2 = nrm_ctr % 2
            nrm_ctr += 1
            rms_rstd(Xs[t].ap, Xs[t], junkc, D, eps6, ssc[i2], rstdc[i2])
            stt("dve", n2[i2].ap, Xs[t].ap, rstdc[i2].ap, gt.ap, ALU.mult, ALU.mult, [Xs[t], rstdc[i2], gt], [n2[i2]])
            tb = 6 + (t % 2)
            psT = bank(tb).bitcast(BF16)
            for c in range(8):
                tr(psT[:, c * 128:(c + 1) * 128], n2[i2].ap[:, c * 128:(c + 1) * 128], ident.ap,
                   [n2[i2], ident], [RB[tb]])
            cp("dve", dstT.ap[:, :, t * 128:(t + 1) * 128],
               psT[:, 0:1024].rearrange("p (c k) -> p c k", k=128), [RB[tb]], [dstT])

    def ff1(W1, AT):
        nonlocal pf_ctr
        for fc in range(4):
            pf = 4 + (pf_ctr % 2)
            RLt = rl_[pf_ctr % 2]
            pf_ctr += 1
            for c in range(8):
                mm(bank(pf), W1.ap[:, c, fc * 128:(fc + 1) * 128], n2T.ap[:, c, :], c == 0, c == 7,
                   [W1, n2T], [RB[pf]])
            act(RLt.ap, bank(pf), AF.Relu, [RB[pf]], [RLt])
            tt("dve", AT.ap[:, fc, :], RLt.ap, RLt.ap, ALU.mult, [RLt], [AT])

    def ff2(W2, AT, Xs):
        nonlocal pa_ctr
        for t in range(4):
            pa = pa_ctr % 2
            pa_ctr += 1
            for nh_ in range(2):
                o = PA[pa][:, nh_ * 512:(nh_ + 1) * 512]
                for fc in range(4):
                    mm(o, AT.ap[:, fc, t * 128:(t + 1) * 128], W2.ap[:, fc, nh_ * 512:(nh_ + 1) * 512],
                       fc == 0, fc == 3, [AT, W2], [RPA[pa][nh_]])
            tt("dve", Xs[t].ap, PA[pa], Xs[t].ap, ALU.add, RPA[pa] + [Xs[t]], [Xs[t]])

    for s in seqs:
        SQ = s["SQ"]
        for g in range(SQ // 512):
            G = gctr % 2
            gctr += 1
            Xs, MA, MB, PL = xh[G], mxa[G], mxb_[G], plt
            for t in range(4):
                tok0 = g * 512 + t * 128
                dma("sp", f"cx{G}", Xs[t].ap, s["x"][tok0:tok0 + 128, :], writes=[Xs[t]])
            for hh in range(4):
                dma("sp", f"cm{G}", MA.ap[:, hh, :], s["MXA"][hh, :, g * 512:(g + 1) * 512], writes=[MA])
            mxb_v = s["MXB"].rearrange("(p2 c) e t -> p2 (c e) t", c=2)
            for hh in range(4):
                dma("sp", f"cm{G}", MB.ap[:, hh, :], mxb_v[hh, :, g * 512:(g + 1) * 512], writes=[MB])
            dma("sp", "cpl", PL.ap, s["pl"][g * 512:(g + 1) * 512, :].rearrange("(t p) e -> p t e", p=128),
                writes=[PL])
            for t in range(4):
                cp("pool", plb.ap, PL.ap[:, t, :], [PL], [plb])
                psT = bank(7).bitcast(BF16)
                for c in range(2):
                    tr(psT[:, c * 128:(c + 1) * 128], plb.ap[:, c * 128:(c + 1) * 128], ident.ap, [plb, ident], [RB[7]])
                cp("dve", pT.ap[:, :, t * 128:(t + 1) * 128], psT[:, 0:256].rearrange("p (c k) -> p c k", k=128),
                   [RB[7]], [pT])
            for t in range(4):
                pa = pa_ctr % 2
                pa_ctr += 1
                for nh_ in range(2):
                    o = PA[pa][:, nh_ * 512:(nh_ + 1) * 512]
                    for c in range(4):
                        mm(o, MA.ap[:, c, t * 128:(t + 1) * 128], wout_a.ap[:, c, nh_ * 512:(nh_ + 1) * 512],
                           c == 0, False, [MA, wout_a], [RPA[pa][nh_]])
                    for c in range(4):
                        mm(o, MB.ap[:, c, t * 128:(t + 1) * 128], wout_a.ap[:, 4 + c, nh_ * 512:(nh_ + 1) * 512],
                           False, c == 3, [MB, wout_a], [RPA[pa][nh_]])
                tt("dve", Xs[t].ap, PA[pa], Xs[t].ap, ALU.add, RPA[pa] + [Xs[t]], [Xs[t]])
            norm_T(Xs, gmlp, n2T)
            ws = []
            for q8 in range(8):
                q4, hf = q8 // 2, q8 % 2
                W1, W2, AT = w1q[wq_ctr % 2], w2q[wq_ctr % 2], a2T[wq_ctr % 2]
                dma("sp", f"w1q{wq_ctr % 2}", W1.ap, W1S[q4][:, :, hf * 512:(hf + 1) * 512], writes=[W1])
                dma("sp", f"w2q{wq_ctr % 2}", W2.ap, W2S[q4][:, hf * 4:(hf + 1) * 4, :], writes=[W2])
                wq_ctr += 1
                ff1(W1, AT)
                if ws:
                    ff2(ws[-1][0], ws[-1][1], Xs)
                ws.append((W2, AT))
            ff2(ws[-1][0], ws[-1][1], Xs)
            norm_T(Xs, gple, n2T)
            for t in range(4):
                pa = pa_ctr % 2
                pa_ctr += 1
                EG = egate[t]
                for nh_ in range(2):
                    o = PA[pa][:, nh_ * 512:(nh_ + 1) * 512]
                    for c in range(8):
                        mm(o, n2T.ap[:, c, t * 128:(t + 1) * 128], wgate.ap[:, c, nh_ * 512:(nh_ + 1) * 512],
                           c == 0, c == 7, [n2T, wgate], [RPA[pa][nh_]])
                act(EG.ap, PA[pa], AF.Sigmoid, RPA[pa], [EG])
            for t in range(4):
                EG = egate[t]
                pa = pa_ctr % 2
                pa_ctr += 1
                for nh_ in range(2):
                    o = PA[pa][:, nh_ * 512:(nh_ + 1) * 512]
                    for c in range(2):
                        mm(o, pT.ap[:, c, t * 128:(t + 1) * 128], wproj.ap[:, c, nh_ * 512:(nh_ + 1) * 512],
                           c == 0, c == 1, [pT, wproj], [RPA[pa][nh_]])
                tt("dve", EG.ap, PA[pa], EG.ap, ALU.mult, RPA[pa] + [EG], [EG])
                tt("pool", Xs[t].ap, Xs[t].ap, EG.ap, ALU.add, [Xs[t], EG], [Xs[t]])
            for t in range(4):
                i2 = nrm_ctr % 2
                nrm_ctr += 1
                YO = yout[t % 2]
                rms_rstd(Xs[t].ap, Xs[t], junkc, D, eps6, ssc[i2], rstdc[i2])
                stt("dve", YO.ap, Xs[t].ap, rstdc[i2].ap, gfin.ap, ALU.mult, ALU.mult, [Xs[t], rstdc[i2], gfin], [YO])
                tok0 = g * 512 + t * 128
                dma("pool", f"yo{t % 2}", s["y"][tok0:tok0 + 128, :], YO.ap, reads=[YO])

    P.emit()
    return nc


def _rope_tables(pos):
    f32 = np.float32
    row = (pos // 64).astype(f32)
    col = (pos % 64).astype(f32)
    inv = (f32(10000.0) ** (-np.arange(0, 32, 2, dtype=f32) / f32(32))).astype(f32)
    ar = (row[:, None] * inv[None]).astype(f32)
    ac = (col[:, None] * inv[None]).astype(f32)
    cr, sr, cc, sc = np.cos(ar), np.sin(ar), np.cos(ac), np.sin(ac)
    C = np.concatenate([cr, cr, cc, cc], axis=1).astype(f32)
    S = np.concatenate([-sr, sr, -sc, sc], axis=1).astype(f32)
    return np.ascontiguousarray(C), np.ascontiguousarray(S)


def _alibi_tables():
    f32 = np.float32
    p = np.arange(128, dtype=np.float64)[:, None]
    x = np.arange(512, dtype=np.float64)[None, :]
    BL = np.zeros((128, 4, 64), f32)
    BR = np.zeros((128, 4, 64), f32)
    FL = np.zeros((128, 4, 512), f32)
    FR = np.zeros((128, 4, 512), f32)
    DT = np.zeros((128, 4, 512), f32)
    m = np.arange(64, dtype=np.float64)[None, :]
    for h, sl in enumerate(SLOPES):
        BL[:, h, :] = -sl * (128.0 * m - p)
        BR[:, h, :] = -sl * (128.0 * m + p + 1.0)
        FL[:, h, :] = np.exp(-sl * x)
        FR[:, h, :] = np.exp(-sl * (511.0 - x))
    THI = np.zeros((128, 4, 512), f32)
    TLO = np.zeros((128, 4, 512), f32)
    for d in range(4):
        dist = np.abs(x - (128.0 * d + p))
        DT[:, d, :] = dist
        THI[:, d, :] = -4.0 * np.floor(dist / 4.0)
        TLO[:, d, :] = -(dist - 4.0 * np.floor(dist / 4.0))
    return [np.ascontiguousarray(a.reshape(128, -1)) for a in (BL, BR, FL, FR, DT, THI, TLO)]


_NC_CACHE = {}


def make_in_maps(inputs):
    xp, xs = np.asarray(inputs["x_prompt"]), np.asarray(inputs["x_sample"])
    pp, psm = np.asarray(inputs["p_prompt"])[0], np.asarray(inputs["p_sample"])[0]
    SKP, SS = xp.shape[1], xs.shape[1]
    SQP = SKP // 2
    f32 = np.float32
    BLt, BRt, FLt, FRt, DTt, THIt, TLOt = _alibi_tables()
    gvec = np.stack([np.asarray(inputs["g_mix"])[0], np.asarray(inputs["g_mlp"])[0],
                     np.asarray(inputs["g_ple"])[0], np.asarray(inputs["g_final"])]).astype(f32)
    lamv = np.stack([np.asarray(inputs[k])[0] for k in ("lambda_q1", "lambda_k1", "lambda_q2", "lambda_k2")]).astype(f32)
    gq, gk = np.asarray(inputs["g_qnorm"])[0], np.asarray(inputs["g_knorm"])[0]
    gqk = np.concatenate([np.tile(gq, 8), np.tile(gk, 2)])[None, :].astype(f32)
    common = {
        "w_in": np.ascontiguousarray(np.asarray(inputs["w_in"])[0]),
        "w_out": np.ascontiguousarray(np.asarray(inputs["w_out"])[0]),
        "w_ff1": np.ascontiguousarray(np.asarray(inputs["w_ff1"])[0]),
        "w_ff2": np.ascontiguousarray(np.asarray(inputs["w_ff2"])[0]),
        "w_gate": np.ascontiguousarray(np.asarray(inputs["w_ple_gate"])[0]),
        "w_proj": np.ascontiguousarray(np.asarray(inputs["w_ple_proj"])[0]),
        "gvec": gvec, "lamv": lamv,
        "gsub": np.ascontiguousarray(np.asarray(inputs["g_subln"])[0][:, None].astype(f32)),
        "gqk": gqk, "ident": np.eye(128, dtype=f32),
        "alibi_bl": BLt, "alibi_br": BRt, "alibi_fl": FLt, "alibi_fr": FRt, "alibi_dt": DTt, "alibi_hi": THIt, "alibi_lo": TLOt,
    }
    pos_nat = np.arange(SKP)
    Cn, Sn = _rope_tables(pos_nat)
    Cr, Sr = _rope_tables(pos_nat[::-1])
    Cs, Ss_ = _rope_tables(np.arange(SS))
    in_maps = []
    for c in range(8):
        b, hf = c // 2, c % 2
        if hf == 0:
            xl, pl = xp[b], pp[b][:SQP]
            rc, rs = Cn, Sn
        else:
            xl, pl = xp[b][::-1], pp[b][::-1][:SQP]
            rc, rs = Cr, Sr
        m = dict(common)
        m.update({
            "x_p": np.ascontiguousarray(xl), "pl_p": np.ascontiguousarray(pl), "ropec_p": rc, "ropes_p": rs,
            "x_s": np.ascontiguousarray(xs[c]), "pl_s": np.ascontiguousarray(psm[c]), "ropec_s": Cs, "ropes_s": Ss_,
        })
        in_maps.append(m)
    return in_maps, (SKP, SQP, SS)


def kernel(**inputs):
    in_maps, cfg = make_in_maps(inputs)
    SKP, SQP, SS = cfg
    if cfg not in _NC_CACHE:
        _NC_CACHE[cfg] = build(SKP, SQP, SS)
    nc = _NC_CACHE[cfg]
    res = run_bass_kernel_spmd(nc, in_maps, core_ids=list(range(8)))
    yp = np.zeros((4, SKP, D), np.float32)
    ys = np.zeros((8, SS, D), np.float32)
    for c in range(8):
        r = res.results[c]
        b, hf = c // 2, c % 2
        if hf == 0:
            yp[b, :SQP] = r["y_p"]
        else:
            yp[b, SQP:] = r["y_p"][::-1]
        ys[c] = r["y_s"]
    return yp, ys
```
